# Optimizing a Trainium2 kernel written in Bass

```python
import jax, jax.numpy as jnp
from jax import lax
import numpy as np

D_MODEL = 4096
BATCH = 4
SEQ = 4096
DEPTH = 1

DN_HEADS = 16
DN_HEAD_DIM = 128
DN_WIDTH = DN_HEADS * DN_HEAD_DIM
DN_CONV = 4
DN_CHUNK = 64
SW_GROUPS = ((128, 1), (512, 4), (2048, 16))
N_SW_GROUPS = 3
SW_HEADS = 8
SW_HEAD_DIM = 128
SW_GROUP_WIDTH = SW_HEADS * SW_HEAD_DIM
ROPE_THETA = 500000.0
ROPE_DIM = SW_HEAD_DIM // 4
N_BRANCHES = 2
D_FF = 11008
FFN_CONV = 3
NORM_EPS = 1e-6
N_MOD = 6
IN_SPLITS = (3 * DN_WIDTH, DN_WIDTH, DN_HEADS, DN_HEADS, 3 * N_SW_GROUPS * SW_GROUP_WIDTH, N_BRANCHES * D_MODEL)
IN_WIDTH = 3 * DN_WIDTH + DN_WIDTH + 2 * DN_HEADS + 3 * N_SW_GROUPS * SW_GROUP_WIDTH + N_BRANCHES * D_MODEL

kernel_name = "hybrid_deltanet_dilated_swa_convffn"


def rms_norm(x, eps=NORM_EPS):
    xf = x.astype(jnp.float32)
    return (xf * lax.rsqrt(jnp.mean(xf * xf, -1, keepdims=True) + eps)).astype(x.dtype)


def l2_norm(x, eps=NORM_EPS):
    return x * lax.rsqrt(jnp.sum(x * x, -1, keepdims=True) + eps)


def causal_dwconv(x, w):
    k_w = w.shape[0]
    t_len = x.shape[1]
    xp = jnp.pad(x, ((0, 0), (k_w - 1, 0), (0, 0)))
    return sum(w[j] * xp[:, j:j + t_len] for j in range(k_w))


def partial_rope(x, positions):
    half = ROPE_DIM // 2
    inv_freq = ROPE_THETA ** (-(jnp.arange(half, dtype=jnp.float32) * 2.0 / ROPE_DIM))
    ang = positions.astype(jnp.float32)[..., None] * inv_freq
    cos = jnp.cos(ang)[:, :, None, :]
    sin = jnp.sin(ang)[:, :, None, :]
    xf = x.astype(jnp.float32)
    x1, x2, rest = xf[..., :half], xf[..., half:ROPE_DIM], xf[..., ROPE_DIM:]
    out = jnp.concatenate([x1 * cos - x2 * sin, x2 * cos + x1 * sin, rest], -1)
    return out.astype(x.dtype)


def dilated_window_attention(q, k, v, dilation, span):
    b, t_len, h, dh = q.shape
    n = t_len // dilation
    nb = -(-n // span)
    npad = nb * span
    bd = b * dilation

    def to_strided(t):
        t = t.reshape(b, n, dilation, h, dh).transpose(0, 2, 1, 3, 4).reshape(bd, n, h, dh)
        return jnp.pad(t, ((0, 0), (0, npad - n), (0, 0), (0, 0)))

    def band(t):
        tb = jnp.pad(t, ((0, 0), (span, 0), (0, 0), (0, 0))).reshape(bd, nb + 1, span, h, dh)
        return jnp.concatenate([tb[:, :-1], tb[:, 1:]], axis=2)

    qb = to_strided(q).reshape(bd, nb, span, h, dh)
    kb = band(to_strided(k))
    vb = band(to_strided(v))
    s = jnp.einsum('znqhd,znkhd->znhqk', qb, kb, preferred_element_type=jnp.float32) * (dh ** -0.5)
    qi = jnp.arange(span)[:, None]
    ki = jnp.arange(2 * span)[None, :]
    rel = span + qi - ki
    blk = jnp.arange(nb)[:, None, None]
    valid = (rel >= 0) & (rel <= span) & (blk * span + ki - span >= 0)
    s = jnp.where(valid[None, :, None], s, -jnp.inf)
    m = jnp.max(s, -1, keepdims=True)
    p = jnp.exp(s - m)
    den = jnp.sum(p, -1, keepdims=True)
    o = jnp.einsum('znhqk,znkhd->znqhd', (p / den).astype(v.dtype), vb)
    lse = (m + jnp.log(den))[..., 0]
    o = o.reshape(bd, npad, h, dh)[:, :n].reshape(b, dilation, n, h, dh)
    o = o.transpose(0, 2, 1, 3, 4).reshape(b, t_len, h, dh)
    lse = lse.transpose(0, 1, 3, 2).reshape(bd, npad, h)[:, :n].reshape(b, dilation, n, h)
    lse = lse.transpose(0, 2, 1, 3).reshape(b, t_len, h)
    return o, lse


def gated_delta_rule(q, k, v, beta, g):
    b, t_len, h, dk = q.shape
    dv = v.shape[-1]
    cs = DN_CHUNK
    n = t_len // cs

    def chunk_vec(t):
        return t.reshape(b, n, cs, h, t.shape[-1]).transpose(0, 1, 3, 2, 4)

    def chunk_scalar(t):
        return t.reshape(b, n, cs, h).transpose(0, 1, 3, 2)

    qc, kc, vc = chunk_vec(q), chunk_vec(k), chunk_vec(v)
    bc, gcum = chunk_scalar(beta), jnp.cumsum(chunk_scalar(g), -1)
    tri = jnp.tril(jnp.ones((cs, cs), bool))
    strict = jnp.tril(jnp.ones((cs, cs), bool), -1)
    gamma = jnp.exp(jnp.where(tri, gcum[..., :, None] - gcum[..., None, :], -jnp.inf))
    kbeta = kc * bc[..., None]
    a_mat = jnp.where(strict, jnp.einsum('bnhid,bnhjd->bnhij', kbeta, kc) * gamma, 0.0)
    lhs = jnp.eye(cs, dtype=jnp.float32) + a_mat
    u = lax.linalg.triangular_solve(lhs, vc * bc[..., None], left_side=True, lower=True)
    w = lax.linalg.triangular_solve(lhs, kbeta * jnp.exp(gcum)[..., None], left_side=True, lower=True)
    aqk = jnp.einsum('bnhid,bnhjd->bnhij', qc, kc) * gamma
    qdec = qc * jnp.exp(gcum)[..., None]
    glast = gcum[..., -1]
    kdec = kc * jnp.exp(glast[..., None] - gcum)[..., None]

    def step(state, xs):
        u_i, w_i, qd_i, aqk_i, kd_i, gl_i = xs
        v_new = u_i - jnp.einsum('bhck,bhkv->bhcv', w_i, state)
        o_i = jnp.einsum('bhck,bhkv->bhcv', qd_i, state) + jnp.einsum('bhij,bhjv->bhiv', aqk_i, v_new)
        state = state * jnp.exp(gl_i)[..., None, None] + jnp.einsum('bhck,bhcv->bhkv', kd_i, v_new)
        return state, o_i

    xs = tuple(jnp.moveaxis(t, 1, 0) for t in (u, w, qdec, aqk, kdec, glast))
    s0 = jnp.zeros((b, h, dk, dv), jnp.float32)
    _, o = lax.scan(step, s0, xs)
    return o.transpose(1, 0, 3, 2, 4).reshape(b, t_len, h, dv)


def setup_inputs(seed: int = 0) -> dict:
    key = jax.random.key(seed)
    ks = jax.random.split(key, 20)
    f32 = jnp.float32
    nrm = lambda k, shape, scale: jax.random.normal(k, shape, f32) * scale
    dt = jnp.exp(jax.random.uniform(ks[9], (DEPTH, DN_HEADS), f32, np.log(1e-3), np.log(1e-1)))
    return {
        "x": nrm(ks[0], (BATCH, SEQ, D_MODEL), 1.0),
        "c": nrm(ks[1], (BATCH, D_MODEL), 1.0),
        "positions": jnp.broadcast_to(jnp.arange(SEQ, dtype=jnp.int32), (BATCH, SEQ)),
        "w_ada": nrm(ks[2], (DEPTH, D_MODEL, N_MOD * D_MODEL), 0.5 * D_MODEL ** -0.5),
        "b_ada": nrm(ks[3], (DEPTH, N_MOD * D_MODEL), 0.02),
        "w_in": nrm(ks[4], (DEPTH, D_MODEL, IN_WIDTH), D_MODEL ** -0.5),
        "b_gate": nrm(ks[5], (DEPTH, N_BRANCHES * D_MODEL), 0.02),
        "conv_qkv": nrm(ks[6], (DEPTH, DN_CONV, 3 * DN_WIDTH), DN_CONV ** -0.5),
        "a_log": jnp.log(jax.random.uniform(ks[7], (DEPTH, DN_HEADS), f32, 1.0, 16.0)),
        "dt_bias": dt + jnp.log(-jnp.expm1(-dt)),
        "o_norm_gain": 1.0 + nrm(ks[10], (DEPTH, DN_HEAD_DIM), 0.02),
        "w_a_proj": nrm(ks[11], (DEPTH, DN_WIDTH, D_MODEL), DN_WIDTH ** -0.5),
        "q_norm_gain": 1.0 + nrm(ks[12], (DEPTH, SW_HEAD_DIM), 0.02),
        "k_norm_gain": 1.0 + nrm(ks[13], (DEPTH, SW_HEAD_DIM), 0.02),
        "w_b_proj": nrm(ks[14], (DEPTH, SW_GROUP_WIDTH, D_MODEL), SW_GROUP_WIDTH ** -0.5),
        "w_o": nrm(ks[15], (DEPTH, D_MODEL, D_MODEL), D_MODEL ** -0.5),
        "w_up": nrm(ks[16], (DEPTH, D_MODEL, 2 * D_FF), D_MODEL ** -0.5),
        "conv_ffn": nrm(ks[17], (DEPTH, FFN_CONV, 2 * D_FF), FFN_CONV ** -0.5),
        "w_down": nrm(ks[18], (DEPTH, D_FF, D_MODEL), D_FF ** -0.5),
    }


def reference(x, c, positions, w_ada, b_ada, w_in, b_gate, conv_qkv, a_log, dt_bias, o_norm_gain,
              w_a_proj, q_norm_gain, k_norm_gain, w_b_proj, w_o, w_up, conv_ffn, w_down):
    b, t_len, _ = x.shape
    split_at = np.cumsum(IN_SPLITS)[:-1].tolist()
    for layer in range(DEPTH):
        mod = (jax.nn.silu(c) @ w_ada[layer] + b_ada[layer])[:, None, :]
        shift_mix, scale_mix, gate_mix, shift_ffn, scale_ffn, gate_ffn = jnp.split(mod, N_MOD, axis=-1)

        h = rms_norm(x) * (1 + scale_mix) + shift_mix
        proj = h @ w_in[layer]
        qkv_a, z_a, b_a, a_a, qkv_b, gate_logits = jnp.split(proj, split_at, axis=-1)

        qkv_a = jax.nn.silu(causal_dwconv(qkv_a, conv_qkv[layer])).astype(jnp.float32)
        qa, ka, va = [t.reshape(b, t_len, DN_HEADS, DN_HEAD_DIM) for t in jnp.split(qkv_a, 3, axis=-1)]
        qa = l2_norm(qa) * (DN_HEAD_DIM ** -0.5)
        ka = l2_norm(ka)
        beta = jax.nn.sigmoid(b_a.astype(jnp.float32))
        g = -jnp.exp(a_log[layer].astype(jnp.float32)) * jax.nn.softplus(
            a_a.astype(jnp.float32) + dt_bias[layer].astype(jnp.float32))
        o_a = gated_delta_rule(qa, ka, va, beta, g).astype(x.dtype)
        o_a = rms_norm(o_a) * o_norm_gain[layer] * jax.nn.silu(z_a.reshape(b, t_len, DN_HEADS, DN_HEAD_DIM))
        y_a = o_a.reshape(b, t_len, DN_WIDTH) @ w_a_proj[layer]

        sw = qkv_b.reshape(b, t_len, N_SW_GROUPS, 3, SW_HEADS, SW_HEAD_DIM)
        outs, lses = [], []
        for gi, (window, dilation) in enumerate(SW_GROUPS):
            qg = partial_rope(rms_norm(sw[:, :, gi, 0]) * q_norm_gain[layer], positions)
            kg = partial_rope(rms_norm(sw[:, :, gi, 1]) * k_norm_gain[layer], positions)
            o_g, lse_g = dilated_window_attention(qg, kg, sw[:, :, gi, 2], dilation, window // dilation)
            outs.append(o_g)
            lses.append(lse_g)
        alpha = jax.nn.softmax(jnp.stack(lses, 0), axis=0)
        o_b = jnp.sum(alpha[..., None] * jnp.stack(outs, 0).astype(jnp.float32), axis=0).astype(x.dtype)
        y_b = o_b.reshape(b, t_len, SW_GROUP_WIDTH) @ w_b_proj[layer]

        gate_a, gate_b = jnp.split(jax.nn.sigmoid(gate_logits + b_gate[layer]), N_BRANCHES, axis=-1)
        x = x + gate_mix * ((gate_a * y_a + gate_b * y_b) @ w_o[layer])

        h = rms_norm(x) * (1 + scale_ffn) + shift_ffn
        up = causal_dwconv(h @ w_up[layer], conv_ffn[layer])
        u_gate, u_val = jnp.split(up, 2, axis=-1)
        x = x + gate_ffn * ((jax.nn.silu(u_gate) * u_val) @ w_down[layer])
    return x
```

```python
import contextlib
import math
import numpy as np
import concourse.bass as bass
import concourse.mybir as mybir
from concourse.bass_utils import run_bass_kernel_spmd

F32 = mybir.dt.float32
BF16 = mybir.dt.bfloat16
I32 = mybir.dt.int32
AF = mybir.ActivationFunctionType
ALU = mybir.AluOpType
AX = mybir.AxisListType


class Cfg:
    def __init__(self, D=4096, NTOK=4096, PRE=2048, HA=16, HS=8, GROUPS=((128, 1), (512, 4), (2048, 16)),
                 DFF=11008, B=4, TB=512, TBI=1024):
        self.D, self.NTOK, self.PRE, self.HA, self.HS, self.GROUPS, self.DFF, self.B, self.TB = \
            D, NTOK, PRE, HA, HS, GROUPS, DFF, B, TB
        self.TBI = TBI
        self.KC = D // 128
        self.NG = len(GROUPS)
        self.DNW = HA * 128
        self.SWW = HS * 128
        self.FC = DFF // 128
        self.OWN = NTOK - PRE
        self.FP0 = PRE - 128
        self.c_qkv = 0
        self.c_z = 3 * self.DNW
        self.c_ba = self.c_z + self.DNW
        self.c_sw = self.c_ba + 2 * HA
        self.c_gate = self.c_sw + 3 * self.NG * self.SWW
        self.INW = self.c_gate + 2 * D
        for (w, d) in GROUPS:
            assert w // d == 128


class Sched:
    ENG = ('pe', 'act', 'dve', 'pool', 'sp')

    def __init__(self, nc, st, nds=8):
        self.nc = nc
        self.prog = {e: [] for e in self.ENG}
        self.sem = {e: st.enter_context(nc.semaphore("sem_" + e)) for e in ('pe', 'act', 'dve', 'pool')}
        self.cnt = {e: 0 for e in ('pe', 'act', 'dve', 'pool')}
        self.nds = nds
        self.dsem = {q: [st.enter_context(nc.semaphore("dsem_%s%d" % (q, i))) for i in range(nds)]
                     for q in ('sp', 'pool', 'act')}
        self.dcnt = {q: 0 for q in ('sp', 'pool', 'act')}
        self.dtok = {q: [None] * nds for q in ('sp', 'pool', 'act')}
        self.seen = {e: {} for e in self.ENG}
        self.bw = {}
        self.br = {}
        self.semobj = {}
        for e, s_ in self.sem.items():
            self.semobj[e] = s_
        for q, l in self.dsem.items():
            for i, s_ in enumerate(l):
                self.semobj[(q, i)] = s_
        self.ninst = {e: 0 for e in self.ENG}

    @staticmethod
    def keys(x):
        if isinstance(x, (str, tuple)):
            return [x]
        return [x.name]

    @staticmethod
    def is_psum(x):
        return (not isinstance(x, (str, tuple))) and 'PSUM' in str(x.space)

    def _deps(self, eng, reads, writes):
        need = {}

        def add(tok):
            if tok is None:
                return
            k, v = tok
            if k == eng and eng == 'pe':
                return
            if self.seen[eng].get(k, 0) >= v:
                return
            if need.get(k, 0) < v:
                need[k] = v
        for r in reads:
            ps_ = self.is_psum(r)
            for kk in self.keys(r):
                add(self.bw.get(kk))
                if ps_:
                    for k, v in self.br.get(kk, {}).items():
                        if k != eng:
                            add((k, v))
        for w in writes:
            for kk in self.keys(w):
                add(self.bw.get(kk))
                for k, v in self.br.get(kk, {}).items():
                    add((k, v))
        for k, v in need.items():
            self.seen[eng][k] = v
        return list(need.items())

    def _commit(self, tok, reads, writes):
        for w in writes:
            for kk in self.keys(w):
                self.bw[kk] = tok
                self.br[kk] = {}
        for r in reads:
            for kk in self.keys(r):
                d = self.br.setdefault(kk, {})
                if d.get(tok[0], 0) < tok[1]:
                    d[tok[0]] = tok[1]

    def op(self, eng, fns, reads, writes):
        if not isinstance(fns, (list, tuple)):
            fns = [fns]
        waits = self._deps(eng, reads, writes)
        self.cnt[eng] += 1
        tok = (eng, self.cnt[eng])
        semobj = self.semobj
        mysem = self.sem[eng]

        def run(e, waits=waits, fns=fns):
            for k, v in waits:
                e.wait_ge(semobj[k], v)
            for f in fns[:-1]:
                f(e)
            fns[-1](e).then_inc(mysem, 1)
        self.prog[eng].append(run)
        self.ninst[eng] += len(fns)
        self.seen[eng][eng] = max(self.seen[eng].get(eng, 0), 0)
        self._commit(tok, reads, writes)
        return tok

    def dma(self, q, out, in_, reads=None, writes=None, **kw):
        reads = [in_] if reads is None else reads
        writes = [out] if writes is None else writes
        i = self.dcnt[q]
        slot = i % self.nds
        waits = self._deps(q, reads, writes)
        prev = self.dtok[q][slot]
        if prev is not None and self.seen[q].get(prev[0], 0) < prev[1]:
            waits.append(prev)
            self.seen[q][prev[0]] = prev[1]
        self.dcnt[q] += 1
        tok = ((q, slot), 16 * (i // self.nds + 1))
        self.dtok[q][slot] = tok
        semobj = self.semobj
        dsem = self.dsem[q][slot]

        def run(e, waits=waits):
            for k, v in waits:
                e.wait_ge(semobj[k], v)
            e.dma_start(out=out, in_=in_, **kw).then_inc(dsem, 16)
        self.prog[q].append(run)
        self.ninst[q] += 1
        self._commit(tok, reads, writes)
        return tok

    def barrier(self):
        toks = {}
        for e, c in self.cnt.items():
            if c > 0:
                toks[e] = c
        for q in self.dtok:
            for tok in self.dtok[q]:
                if tok is not None and toks.get(tok[0], 0) < tok[1]:
                    toks[tok[0]] = tok[1]
        semobj = self.semobj
        for eng in self.ENG:
            items = [(k_, v) for k_, v in toks.items() if self.seen[eng].get(k_, 0) < v and not (k_ == eng and eng == 'pe')]
            for k_, v in items:
                self.seen[eng][k_] = v

            def run(e, items=items):
                for k_, v in items:
                    e.wait_ge(semobj[k_], v)
            self.prog[eng].append(run)

    def final_wait(self, eng='sp'):
        toks = {}
        for tok in list(self.bw.values()):
            if tok is not None and toks.get(tok[0], 0) < tok[1]:
                toks[tok[0]] = tok[1]
        for q in self.dtok:
            for tok in self.dtok[q]:
                if tok is not None and toks.get(tok[0], 0) < tok[1]:
                    toks[tok[0]] = tok[1]
        semobj = self.semobj
        items = list(toks.items())

        def run(e):
            for k, v in items:
                e.wait_ge(semobj[k], v)
        self.prog[eng].append(run)

    def emit(self, block):
        prog = self.prog

        @block.tensor
        def _(e):
            for f in prog['pe']:
                f(e)

        @block.scalar
        def _(e):
            for f in prog['act']:
                f(e)

        @block.vector
        def _(e):
            for f in prog['dve']:
                f(e)

        @block.gpsimd
        def _(e):
            for f in prog['pool']:
                f(e)

        @block.sync
        def _(e):
            for f in prog['sp']:
                f(e)


class K:
    def __init__(self, nc, st, cfg, debug=()):
        self.nc, self.st, self.cfg = nc, st, cfg
        self.s = Sched(nc, st)
        self.debug = set(debug)
        self.rings = {}
        self.ps = [st.enter_context(nc.psum_tensor("psb%d" % i, [128, 512], F32)) for i in range(8)]
        self.uid = 0
        self.castrr = 0
        self.preloaded = {}

    def sb(self, name, shape, dtype):
        stack = self.cur if getattr(self, 'cur', None) is not None else self.st
        return stack.enter_context(self.nc.sbuf_tensor(name, list(shape), dtype))

    def ring(self, name, shape, dtype, n):
        self.uid += 1
        self.rings[name] = [[self.sb("%s_u%d_%d" % (name, self.uid, i), shape, dtype) for i in range(n)], 0]

    def nxt(self, name):
        r = self.rings[name]
        t = r[0][r[1] % len(r[0])]
        r[1] += 1
        return t

    def dram(self, name, shape, dtype):
        kind = "ExternalOutput" if name in self.debug else "Internal"
        return self.nc.dram_tensor(name, list(shape), dtype, kind=kind).ap()

    def dump(self, name, ap, dtype=F32):
        if name in self.debug:
            d = self.nc.dram_tensor(name, list(ap.shape), dtype, kind="ExternalOutput").ap()
            self.dma(d, ap)

    @staticmethod
    def _aps(*xs):
        return [x for x in xs if x is not None and hasattr(x, 'name') and hasattr(x, 'ap')]

    def act(self, out, in_, func, bias=None, scale=None, accum=None, eng='act'):
        kw = {}
        if bias is not None:
            kw['bias'] = bias
        if scale is not None:
            kw['scale'] = scale
        if accum is not None:
            kw['accum_out'] = accum
        return self.s.op('act', lambda e: e.activation(out=out, in_=in_, func=func, **kw),
                         self._aps(in_, bias, scale), self._aps(out, accum))

    def ts(self, out, in0, s1, s2=None, op0=ALU.mult, op1=None, eng='dve', accum=None):
        kw = {}
        if op1 is not None:
            kw['op1'] = op1
        if accum is not None:
            kw['accum_out'] = accum
        return self.s.op(eng, lambda e: e.tensor_scalar(out=out, in0=in0, scalar1=s1, scalar2=s2, op0=op0, **kw),
                         self._aps(in0, s1, s2), self._aps(out, accum))

    def stt(self, out, in0, scalar, in1, op0, op1):
        return self.s.op('dve', lambda e: e.scalar_tensor_tensor(out=out, in0=in0, scalar=scalar, in1=in1,
                                                                  op0=op0, op1=op1),
                         self._aps(in0, scalar, in1), self._aps(out))

    def tt(self, out, in0, in1, op, eng='dve'):
        return self.s.op(eng, lambda e: e.tensor_tensor(out=out, in0=in0, in1=in1, op=op),
                         self._aps(in0, in1), self._aps(out))

    def copy(self, out, in_, eng='dve'):
        if eng == 'act':
            return self.s.op('act', lambda e: e.copy(out=out, in_=in_), self._aps(in_), self._aps(out))
        return self.s.op(eng, lambda e: e.tensor_copy(out=out, in_=in_), self._aps(in_), self._aps(out))

    def memset(self, ap, val, eng='dve'):
        return self.s.op(eng, lambda e: e.memset(ap, val), [], self._aps(ap))

    def recip(self, out, in_):
        return self.s.op('dve', lambda e: e.reciprocal(out=out, in_=in_), self._aps(in_), self._aps(out))

    def reduce(self, out, in_, op=ALU.add, axis=AX.X):
        return self.s.op('dve', lambda e: e.tensor_reduce(out=out, in_=in_, axis=axis, op=op),
                         self._aps(in_), self._aps(out))

    def mm(self, out, pairs, extra_writes=()):
        n = len(pairs)
        fns = []
        reads = []
        for i, (l, r) in enumerate(pairs):
            fns.append(lambda e, l=l, r=r, i=i: e.matmul(out, l, r, start=(i == 0), stop=(i == n - 1)))
            reads += [l, r]
        return self.s.op('pe', fns, self._aps(*reads), self._aps(out))

    def mm_multi(self, groups):
        fns, reads, writes = [], [], []
        for out, pairs in groups:
            n = len(pairs)
            for i, (l, r) in enumerate(pairs):
                fns.append(lambda e, out=out, l=l, r=r, i=i, n=n: e.matmul(out, l, r, start=(i == 0), stop=(i == n - 1)))
                reads += [l, r]
            writes.append(out)
        return self.s.op('pe', fns, self._aps(*reads), self._aps(*writes))

    def tr(self, outs_ins, ident):
        fns, reads, writes = [], [ident], []
        for o, i_ in outs_ins:
            fns.append(lambda e, o=o, i_=i_: e.transpose(o, i_, ident))
            reads.append(i_)
            writes.append(o)
        return self.s.op('pe', fns, self._aps(*reads), self._aps(*writes))

    def dma(self, out, in_, q='sp', **kw):
        return self.s.dma(q, out, in_, **kw)


SLAB_ELEMS = 16384


PIECE = 4096


def gemm(k, kind, w_ap, krows, col_ranges, act_fn, tok_ranges, epi, psum_banks, prefetch_next=None):
    kcn = krows // 128
    assert kcn * 128 == krows
    pend = []
    bank_i = [0]
    npieces_ring = len(k.rings['wp'][0])

    def advance():
        for g in list(pend):
            try:
                next(g)
            except StopIteration:
                pend.remove(g)

    def run_epi(*a):
        advance()
        g = epi(*a)
        if g is not None:
            pend.append(g)
            try:
                next(g)
            except StopIteration:
                pend.remove(g)

    wv = w_ap.rearrange("(kc p) n -> p kc n", p=128)

    def load_piece(c0, ncols, pi):
        key_ = (w_ap.name, c0, ncols, pi)
        if key_ in k.preloaded:
            return k.preloaded.pop(key_)
        pk = max(1, PIECE // ncols)
        ka = pi * pk
        pn = min(pk, kcn - ka)
        stg = k.nxt('wstage')
        stv = stg[:, 0:pn * ncols].rearrange("p (kc n) -> p kc n", n=ncols)
        k.dma(stv, wv[:, ka:ka + pn, c0:c0 + ncols], q='sp')
        pc = k.nxt('wp')
        pv = pc[:, 0:pn * ncols].rearrange("p (kc n) -> p kc n", n=ncols)
        ce = ('dve', 'dve', 'act', 'dve')[k.castrr % 4]
        k.castrr += 1
        k.copy(pv, stv, eng=ce)
        return pv, pn

    items = []
    for r, (c0, ncols, tag) in enumerate(col_ranges):
        assert ncols <= 512
        pk = max(1, PIECE // ncols)
        npc = -(-kcn // pk)
        for pi in range(npc):
            items.append((r, pi))
    npc_max = max(-(-kcn // max(1, PIECE // nc_)) for (_, nc_, _) in col_ranges)
    resident = npc_max * 2 <= npieces_ring
    pf = npc_max if resident else max(1, npieces_ring - 2)
    loaded = {}
    nxt_load = [0]

    def ensure(upto):
        while nxt_load[0] < min(upto, len(items)):
            r_, pi_ = items[nxt_load[0]]
            c0_, ncols_, _ = col_ranges[r_]
            loaded[(r_, pi_)] = load_piece(c0_, ncols_, pi_)
            nxt_load[0] += 1

    pos = 0
    for r, (c0, ncols, tag) in enumerate(col_ranges):
        pk = max(1, PIECE // ncols)
        npc = -(-kcn // pk)
        if r == len(col_ranges) - 1 and prefetch_next is not None:
            ensure(len(items))
            prefetch_next()
        if resident:
            ensure(pos + npc + pf)
            pcs = [loaded.pop((r, pi)) for pi in range(npc)]
            pos += npc

            def wsl(kc, pcs=pcs, pk=pk):
                return pcs[kc // pk][0][:, kc % pk, :]
            if kind == 'T':
                for (t0, nt) in tok_ranges:
                    bank = psum_banks[bank_i[0] % len(psum_banks)]
                    bank_i[0] += 1
                    o = bank[0:nt, 0:ncols]
                    fns, reads = [], []
                    for j in range(kcn):
                        l = act_fn(j, t0, nt)
                        r_ = wsl(j)
                        fns.append(lambda e, o=o, l=l, r_=r_, st_=(j == 0), sp_=(j == kcn - 1):
                                   e.matmul(o, l, r_, start=st_, stop=sp_))
                        reads += [l, r_]
                    k.s.op('pe', fns, k._aps(*reads), k._aps(o))
                    run_epi(tag, c0, ncols, t0, nt, o)
            else:
                for cj in range(0, ncols, 128):
                    cw = min(128, ncols - cj)
                    for (t0, nt) in tok_ranges:
                        bank = psum_banks[bank_i[0] % len(psum_banks)]
                        bank_i[0] += 1
                        o = bank[0:cw, 0:nt]
                        fns, reads = [], []
                        for j in range(kcn):
                            l = wsl(j)[:, cj:cj + cw]
                            r_ = act_fn(j, t0, nt)
                            fns.append(lambda e, o=o, l=l, r_=r_, st_=(j == 0), sp_=(j == kcn - 1):
                                       e.matmul(o, l, r_, start=st_, stop=sp_))
                            reads += [l, r_]
                        k.s.op('pe', fns, k._aps(*reads), k._aps(o))
                        run_epi(tag, c0 + cj, cw, t0, nt, o)
        else:
            assert kind == 'T' and len(tok_ranges) <= len(psum_banks)
            outs = [psum_banks[ti][0:nt, 0:ncols] for ti, (t0, nt) in enumerate(tok_ranges)]
            for pi in range(npc):
                ensure(pos + 1 + pf)
                pv, pn = loaded.pop((r, pi))
                pos += 1
                ka = pi * pk
                for ti, (t0, nt) in enumerate(tok_ranges):
                    o = outs[ti]
                    fns, reads = [], []
                    for j in range(pn):
                        l = act_fn(ka + j, t0, nt)
                        r_ = pv[:, j, :]
                        fns.append(lambda e, o=o, l=l, r_=r_, st_=(pi == 0 and j == 0), sp_=(pi == npc - 1 and j == pn - 1):
                                   e.matmul(o, l, r_, start=st_, stop=sp_))
                        reads += [l, r_]
                    k.s.op('pe', fns, k._aps(*reads), k._aps(o))
                    if pi == npc - 1:
                        run_epi(tag, c0, ncols, t0, nt, o)
    while pend:
        advance()


def run_jobs(k, jobs):
    for i, j in enumerate(jobs):
        nxt = jobs[i + 1] if i + 1 < len(jobs) else None

        def pf(nxt=nxt):
            if nxt is None:
                return
            c0, ncols, _ = nxt['cr'][0]
            kcn = nxt['krows'] // 128
            pk = max(1, PIECE // ncols)
            npc = min(-(-kcn // pk), 4)
            wv = nxt['w'].rearrange("(kc p) n -> p kc n", p=128)
            for pi in range(npc):
                ka = pi * pk
                pn = min(pk, kcn - ka)
                stg = k.nxt('wstage')
                stv = stg[:, 0:pn * ncols].rearrange("p (kc n) -> p kc n", n=ncols)
                k.dma(stv, wv[:, ka:ka + pn, c0:c0 + ncols], q='sp')
                pc = k.nxt('wp')
                pv = pc[:, 0:pn * ncols].rearrange("p (kc n) -> p kc n", n=ncols)
                ce = ('dve', 'dve', 'act', 'dve')[k.castrr % 4]
                k.castrr += 1
                k.copy(pv, stv, eng=ce)
                k.preloaded[(nxt['w'].name, c0, ncols, pi)] = (pv, pn)
        gemm(k, j['kind'], j['w'], j['krows'], j['cr'], j['act_fn'], j['toks'], j['epi'], j['banks'],
             prefetch_next=(pf if nxt is not None else None))


EPS = 1e-6
TWO_PI = 2.0 * math.pi
CW1 = 6.28125
CW2 = TWO_PI - CW1
NEG = -30000.0


def host_consts(cfg):
    c = {}
    c['identF'] = np.eye(128, dtype=np.float32)
    c['onesF'] = np.ones((128, 128), dtype=np.float32)
    ii = np.arange(128)
    c['nm_incl'] = np.where(ii[None, :] >= ii[:, None], 0.0, NEG).astype(np.float32)
    c['m_sneg'] = -(ii[None, :] > ii[:, None]).astype(np.float32)
    c['tri_incl'] = (ii[:, None] <= ii[None, :]).astype(np.float32)
    mm_ = [(ii[:, None] // 16 == ii[None, :] // 16)]
    offs = [((ii[:, None] // b) % 2 == 0) & (ii[None, :] // b == ii[:, None] // b + 1) for b in (16, 32, 64)]
    mm_ += offs + [o.T for o in offs]
    c['binv'] = np.ascontiguousarray(np.concatenate([m.astype(np.float32) for m in mm_], 1))
    half = 16
    inv = (np.float32(500000.0) ** (-(np.arange(half, dtype=np.float32) * np.float32(2.0) / np.float32(32)))).astype(np.float32)
    c['invf'] = np.ascontiguousarray(np.broadcast_to(inv[None, :], (128, half))).astype(np.float32)
    ms = []
    idx = []
    for g, (w, dl) in enumerate(cfg.GROUPS):
        for j in range(dl + 1):
            diff = 128 * j + ii[None, :] - ii[:, None]
            m = (diff >= 0) & (diff % dl == 0) & (diff <= 128 * dl)
            ms.append(m.astype(np.float32))
            idx.append((g, j))
    c['amask'] = np.ascontiguousarray(np.stack(ms, 1).reshape(128, -1)).astype(np.float32)
    cfg.mask_idx = {gj: i for i, gj in enumerate(idx)}
    return c


def build(cfg, debug=()):
    nc = bass.Bass("TRN2", target_bir_lowering=False)
    st = contextlib.ExitStack()
    with st:
        k = K(nc, st, cfg, debug)
        _build_body(nc, st, k, cfg)
    return nc


def _inp(nc, name, shape, dtype=F32):
    return nc.dram_tensor(name, list(shape), dtype, kind="ExternalInput").ap()


def _build_body(nc, st, k, cfg):
    D, KC, NTOK, PRE, TB, HA, HS, NG, DFF, FC = cfg.D, cfg.KC, cfg.NTOK, cfg.PRE, cfg.TB, cfg.HA, cfg.HS, cfg.NG, cfg.DFF, cfg.FC
    NT = NTOK // 128
    TBI = cfg.TBI
    NLB = NTOK // TBI
    TPB = TB // 128
    TPBI = TBI // 128
    OWN = cfg.OWN
    NOB = OWN // TB
    LB0 = PRE // TBI
    HT = PRE // 128 - 1
    NFPT = OWN // 128 + 1
    NFPTOK = NFPT * 128
    DNW, SWW = cfg.DNW, cfg.SWW
    hc = host_consts(cfg)
    NMASK = hc['amask'].shape[1] // 128

    x_loc = _inp(nc, "x_loc", [NTOK, D])
    cT_in = _inp(nc, "cT", [128, KC])
    pos_in = _inp(nc, "posT", [128, NT], I32)
    flag_in = _inp(nc, "flagP", [128, 2])
    w_ada = _inp(nc, "w_ada", [D, 6 * D])
    b_ada = _inp(nc, "b_ada", [1, 6 * D])
    w_in = _inp(nc, "w_in", [D, cfg.INW])
    bgate_in = _inp(nc, "b_gateT", [128, 2 * KC])
    convq_in = _inp(nc, "conv_qkvT", [128, 3 * HA * 4])
    alog_in = _inp(nc, "a_log_bc", [128, HA])
    dtb_in = _inp(nc, "dt_bias_bc", [128, HA])
    ogain_in = _inp(nc, "o_gain_bc", [128, 128])
    qgain_in = _inp(nc, "q_gain_bc", [128, 512])
    kgain_in = _inp(nc, "k_gain_bc", [128, 512])
    w_a = _inp(nc, "w_a_proj", [DNW, D])
    w_b = _inp(nc, "w_b_proj", [SWW, D])
    w_o = _inp(nc, "w_o", [D, D])
    w_up = _inp(nc, "w_up", [D, 2 * DFF])
    convf_in = _inp(nc, "conv_ffnT", [128, 2 * FC * 3])
    w_down = _inp(nc, "w_down", [DFF, D])
    cin = {n: _inp(nc, "c_" + n, list(a.shape)) for n, a in hc.items()}
    out = nc.dram_tensor("out", [OWN, D], F32, kind="ExternalOutput").ap()

    identF = k.sb("identF", [128, 128], F32)
    identB = k.sb("identB", [128, 128], BF16)
    onesF = k.sb("onesF", [128, 128], F32)
    onesB = k.sb("onesB", [128, 128], BF16)
    flag = k.sb("flag", [128, 2], F32)
    modT = k.sb("modT", [128, 6 * KC], F32)
    k.dma(identF[:], cin['identF'])
    k.dma(onesF[:], cin['onesF'])
    k.dma(flag[:], flag_in)
    k.copy(identB[:], identF[:])
    k.copy(onesB[:], onesF[:])
    PS = k.ps

    cact = k.sb("cact", [128, KC], BF16)
    gate_bc = [k.dram("gate_bc%d" % i, [128, D], F32) for i in range(2)]
    with contextlib.ExitStack() as st0:
        k.cur = st0
        k.ring('wp', [128, PIECE], BF16, 8)
        k.ring('wstage', [128, PIECE], F32, 4)
        ctmp = st0.enter_context(nc.sbuf_tensor("ctmp", [128, KC], F32))
        rowsb = [st0.enter_context(nc.sbuf_tensor("rowsb%d" % i, [1, 512], F32)) for i in range(2)]
        bpc = [st0.enter_context(nc.sbuf_tensor("bpc%d" % i, [1, 512], F32)) for i in range(2)]
        gtile = [st0.enter_context(nc.sbuf_tensor("gtile%d" % i, [128, 512], F32)) for i in range(2)]
        k.dma(ctmp[:], cT_in)
        k.act(cact[:], ctmp[:], AF.Silu)
        cnt = [0]

        def epi0(tag, c0, ncols, t0, nt, o):
            i = cnt[0] % 2
            cnt[0] += 1
            k.dma(bpc[i][0:1, 0:ncols], b_ada[0:1, c0:c0 + ncols])
            k.tt(rowsb[i][0:1, 0:ncols], o, bpc[i][0:1, 0:ncols], ALU.add)
            yield
            pc = PS[6][:, 0:8]
            k.mm_multi([(pc[:, 2 * j:2 * j + 2], [(rowsb[i][0:1, j * 128:(j + 1) * 128], onesF[0:1, 0:2])])
                        for j in range(ncols // 128)])
            ch0 = c0 // 128
            k.copy(modT[:, ch0:ch0 + ncols // 128],
                   pc.rearrange("p (j t) -> p j t", t=2)[:, 0:ncols // 128, 0])
            which = c0 // D
            if which in (2, 5):
                pb = PS[7][:, 0:ncols]
                k.mm(pb, [(onesF[0:1, 0:128], rowsb[i][0:1, 0:ncols])])
                cc = c0 - which * D
                gt_ = gtile[i]
                k.copy(gt_[:, 0:ncols], pb, eng='act')
                k.dma(gate_bc[0 if which == 2 else 1][:, cc:cc + ncols], gt_[:, 0:ncols])

        cw = min(512, D)
        gemm(k, 'T', w_ada, D, [(c0, cw, 0) for c0 in range(0, 6 * D, cw)],
             lambda kc, t0, nt: cact[:, kc:kc + 1], [(0, 1)], epi0, [PS[0], PS[1]])
        k.s.barrier()
        k.cur = None
    k.ts(modT[:, KC:2 * KC], modT[:, KC:2 * KC], 1.0, None, op0=ALU.add)
    k.ts(modT[:, 4 * KC:5 * KC], modT[:, 4 * KC:5 * KC], 1.0, None, op0=ALU.add)
    k.dump("dbg_modT", modT[:])

    def norm_stage(tag, tiles, src_fn, scale_ap, shift_ap, dst_fn):
        with contextlib.ExitStack() as stn:
            xt = [stn.enter_context(nc.sbuf_tensor("%s_xt%d" % (tag, i), [128, D], F32)) for i in range(2)]
            junk = stn.enter_context(nc.sbuf_tensor(tag + "_junk", [128, D], BF16))
            ht = [stn.enter_context(nc.sbuf_tensor("%s_ht%d" % (tag, i), [128, KC, 128], BF16)) for i in range(2)]
            sm = [stn.enter_context(nc.sbuf_tensor("%s_sm%d" % (tag, i), [128, 4], F32)) for i in range(2)]
            for n, t in enumerate(tiles):
                x_, h_, s_ = xt[n % 2], ht[n % 2], sm[n % 2]
                k.dma(x_[:], src_fn(t))
                k.act(junk[:], x_[:], AF.Square, accum=s_[:, 0:1])
                k.act(s_[:, 1:2], s_[:, 0:1], AF.Sqrt, bias=EPS, scale=1.0 / D)
                k.recip(s_[:, 2:3], s_[:, 1:2])
                k.ts(x_[:], x_[:], s_[:, 2:3], None, op0=ALU.mult)
                for kc0 in range(0, KC, 4):
                    bank = PS[2 + (kc0 // 4) % 2]
                    k.tr([(bank[:, j * 128:(j + 1) * 128], x_[:, (kc0 + j) * 128:(kc0 + j + 1) * 128])
                          for j in range(4)], identF[:])
                    for j in range(4):
                        kc = kc0 + j
                        if (kc0 // 4) % 2 == 0:
                            k.act(h_[:, kc, :], bank[:, j * 128:(j + 1) * 128], AF.Identity,
                                  bias=shift_ap[:, kc:kc + 1], scale=scale_ap[:, kc:kc + 1])
                        else:
                            k.ts(h_[:, kc, :], bank[:, j * 128:(j + 1) * 128], scale_ap[:, kc:kc + 1],
                                 shift_ap[:, kc:kc + 1], op0=ALU.mult, op1=ALU.add)
                k.dma(dst_fn(t), h_[:])
            k.s.barrier()

    hT1 = [k.dram("hT1_b%d" % b, [128, KC, TBI], BF16) for b in range(NLB)]
    norm_stage("n1", list(range(NT)), lambda t: x_loc[t * 128:(t + 1) * 128, :],
               modT[:, KC:2 * KC], modT[:, 0:KC],
               lambda t: hT1[t // TPBI][:, :, (t % TPBI) * 128:(t % TPBI + 1) * 128])
    k.cfgvals = dict(NT=NT, NLB=NLB, TPB=TPB, TPBI=TPBI, NOB=NOB, LB0=LB0, HT=HT, NFPT=NFPT, NFPTOK=NFPTOK, NMASK=NMASK)
    _stage_inproj(nc, k, cfg, locals())


def _stage_inproj(nc, k, cfg, L):
    D, KC, NTOK, PRE, TB, HA, HS, NG = cfg.D, cfg.KC, cfg.NTOK, cfg.PRE, cfg.TB, cfg.HA, cfg.HS, cfg.NG
    DNW, SWW = cfg.DNW, cfg.SWW
    V = k.cfgvals
    NT, NLB, TPB, NOB, LB0, HT, NFPT, NFPTOK = V['NT'], V['NLB'], V['TPBI'], V['NOB'], V['LB0'], V['HT'], V['NFPT'], V['NFPTOK']
    TB = cfg.TBI
    PS, flag, identB, identF, onesF = k.ps, L['flag'], L['identB'], L['identF'], L['onesF']
    w_in, hT1, cin = L['w_in'], L['hT1'], L['cin']
    NQC = 3 * HA
    S = {}
    S['dnq'] = [k.dram("dnq_b%d" % b, [128, NQC, TB], BF16) for b in range(NLB)]
    S['bg'] = k.dram("bg", [NT, 128, 2 * HA], F32)
    S['zs'] = k.dram("zs", [NFPT, 128, DNW], F32)
    S['swk'] = [k.dram("swk_g%d" % g, [128, HS, NTOK], BF16) for g in range(NG)]
    S['swq'] = [k.dram("swq_g%d" % g, [128, HS, NFPTOK], BF16) for g in range(NG)]
    S['swv'] = [k.dram("swv_g%d" % g, [NT, 128, HS, 128], BF16) for g in range(NG)]
    S['gT'] = k.dram("gT", [2 * KC, 128, NFPTOK], BF16)
    k.S = S
    with contextlib.ExitStack() as sx:
        k.cur = sx
        sbt = lambda name, shape, dt: sx.enter_context(nc.sbuf_tensor(name, list(shape), dt))
        cosT = sbt("ip_cos", [128, NT, 16], F32)
        sinT = sbt("ip_sin", [128, NT, 16], F32)
        with contextlib.ExitStack() as sr:
            sbr = lambda name, shape, dt: sr.enter_context(nc.sbuf_tensor(name, list(shape), dt))
            invf = sbr("ip_invf", [128, 16], F32)
            posi = sbr("ip_posi", [128, NT], I32)
            posf = sbr("ip_posf", [128, NT], F32)
            rtmp = sbr("ip_rtmp", [128, NT, 16], F32)
            rtmp2 = sbr("ip_rtmp2", [128, NT, 16], F32)
            rtmpi = sbr("ip_rtmpi", [128, NT, 16], I32)
            k.dma(invf[:], cin['invf'])
            k.dma(posi[:], L['pos_in'])
            k.copy(posf[:], posi[:])
            k.tt(rtmp[:], posf[:].unsqueeze(2).broadcast_to([128, NT, 16]),
                 invf[:].unsqueeze(1).broadcast_to([128, NT, 16]), ALU.mult)

            def sin_of(dst, shift):
                k.ts(rtmp2[:], rtmp[:], shift, 1.0 / TWO_PI, op0=ALU.add, op1=ALU.mult)
                k.copy(rtmpi[:], rtmp2[:])
                k.copy(rtmp2[:], rtmpi[:])
                k.ts(dst, rtmp[:], shift, None, op0=ALU.add)
                k.stt(dst, rtmp2[:], -CW1, dst, ALU.mult, ALU.add)
                k.stt(dst, rtmp2[:], -CW2, dst, ALU.mult, ALU.add)
                k.ts(rtmp2[:], dst, math.pi, -TWO_PI, op0=ALU.is_gt, op1=ALU.mult)
                k.tt(dst, dst, rtmp2[:], ALU.add)
                k.ts(rtmp2[:], dst, -math.pi, TWO_PI, op0=ALU.is_lt, op1=ALU.mult)
                k.tt(dst, dst, rtmp2[:], ALU.add)
                k.act(dst, dst, AF.Sin)
            sin_of(sinT[:], 0.0)
            sin_of(cosT[:], math.pi / 2)
            k.dump("dbg_cos", cosT[:].rearrange("p a b -> p (a b)"))
            k.dump("dbg_sin", sinT[:].rearrange("p a b -> p (a b)"))
            k.s.barrier()
        k.ring('wp', [128, PIECE], BF16, 8)
        k.ring('wstage', [128, PIECE], F32, 2)
        actb = sbt("ip_act", [128, KC, TB], BF16)
        acth = sbt("ip_acth", [128, KC, 128], BF16)
        convw = sbt("ip_convw", [128, NQC, 4], F32)
        dnhalo = sbt("ip_dnhalo", [128, NQC, 3], F32)
        bgT = sbt("ip_bgT", [128, 2 * KC], F32)
        alog = sbt("ip_alog", [128, HA], F32)
        dtb = sbt("ip_dtb", [128, HA], F32)
        negA = sbt("ip_negA", [128, HA], F32)
        qgain = sbt("ip_qgain", [128, 512], F32)
        kgain = sbt("ip_kgain", [128, 512], F32)
        k.dma(convw[:].rearrange("p a b -> p (a b)"), L['convq_in'])
        k.dma(bgT[:], L['bgate_in'])
        k.dma(alog[:], L['alog_in'])
        k.dma(dtb[:], L['dtb_in'])
        k.dma(qgain[:], L['qgain_in'])
        k.dma(kgain[:], L['kgain_in'])
        k.memset(dnhalo[:], 0.0)
        k.act(negA[:], alog[:], AF.Exp)
        k.ts(negA[:], negA[:], -1.0, None, op0=ALU.mult)

        k.ring('ip_xr', [128, 512 + 4], F32, 2)
        k.ring('ip_acc', [128, 512], F32, 2)
        k.ring('ip_ob', [128, 512], BF16, 3)
        k.ring('ip_t512', [128, 512], F32, 3)
        k.ring('ip_small', [128, 64], F32, 3)
        k.ring('ip_qb', [128, 512], BF16, 7)
        k.ring('ip_qT', [128, 512], BF16, 2)
        ecnt = [0]

        for lb in range(NLB):
            own = lb >= LB0
            first_own = (lb == LB0)
            k.dma(actb[:], hT1[lb])
            if first_own:
                k.dma(acth[:], hT1[lb - 1][:, :, TB - 128:TB])
            t_base = lb * TB
            jobs = []

            def act_fn(kc, t0, nt):
                if t0 < 0:
                    return acth[:, kc, t0 + 128:t0 + 128 + nt]
                return actb[:, kc, t0:t0 + nt]

            def epi_dn(tag, c0, ncols, t0, nt, o):
                c = c0 // 128
                xr = k.nxt('ip_xr')
                acc = k.nxt('ip_acc')
                ob = k.nxt('ip_ob')
                k.copy(xr[:, 0:3], dnhalo[:, c, :], eng='pool')
                if own:
                    k.copy(xr[:, 3:3 + nt], o, eng='act')
                else:
                    k.act(xr[:, 3:3 + nt], o, AF.Copy, scale=flag[:, 0:1])
                k.copy(dnhalo[:, c, :], xr[:, nt:nt + 3], eng='pool')
                k.ts(acc[:, 0:nt], xr[:, 0:nt], convw[:, c, 0:1], None, op0=ALU.mult)
                for j in (1, 2, 3):
                    k.stt(acc[:, 0:nt], xr[:, j:j + nt], convw[:, c, j:j + 1], acc[:, 0:nt], ALU.mult, ALU.add)
                k.act(ob[:, 0:nt], acc[:, 0:nt], AF.Silu)
                k.dma(S['dnq'][lb][:, c, t0:t0 + nt], ob[:, 0:nt])
                return None
                yield

            jobs.append(dict(kind='F', w=w_in, krows=D, cr=[(c0, min(512, 3 * DNW - c0), 0) for c0 in range(0, 3 * DNW, 512)],
                             act_fn=act_fn, toks=[(t_, 512) for t_ in range(0, TB, 512)], epi=epi_dn, banks=[PS[0], PS[1], PS[2], PS[3]]))

            def epi_ba(tag, c0, ncols, t0, nt, o):
                tl = (t_base + t0) // 128
                sm = k.nxt('ip_small')
                k.act(sm[:, 0:HA], o[:, 0:HA], AF.Sigmoid)
                k.tt(sm[:, HA:2 * HA], o[:, HA:2 * HA], dtb[:], ALU.add)
                k.act(sm[:, HA:2 * HA], sm[:, HA:2 * HA], AF.Exp)
                k.act(sm[:, HA:2 * HA], sm[:, HA:2 * HA], AF.Ln, bias=1.0, scale=1.0)
                k.tt(sm[:, HA:2 * HA], sm[:, HA:2 * HA], negA[:], ALU.mult)
                k.dma(S['bg'][tl], sm[:, 0:2 * HA])
                return None
                yield

            ba_job = dict(kind='T', w=w_in, krows=D, cr=[(cfg.c_ba, 2 * HA, 0)], act_fn=act_fn,
                          toks=[(i * 128, 128) for i in range(TPB)], epi=epi_ba, banks=[PS[0], PS[1], PS[2], PS[3]])

            def epi_qk(tag, c0, ncols, t0, nt, o):
                g, which, h0 = tag
                tl = (t_base + t0) // 128
                gain = qgain if which == 0 else kgain
                sq = k.nxt('ip_t512')
                sm = k.nxt('ip_small')
                xn = k.nxt('ip_t512')
                qb = k.nxt('ip_qb')
                k.act(sq[:], o, AF.Square)
                k.reduce(sm[:, 0:4], sq[:].rearrange("p (h d) -> p h d", d=128))
                k.act(sm[:, 4:8], sm[:, 0:4], AF.Sqrt, bias=EPS, scale=1.0 / 128)
                k.recip(sm[:, 8:12], sm[:, 4:8])
                k.tt(xn[:].rearrange("p (h d) -> p h d", d=128), o.rearrange("p (h d) -> p h d", d=128),
                     sm[:, 8:12].unsqueeze(2).broadcast_to([128, 4, 128]), ALU.mult)
                k.tt(xn[:], xn[:], gain[:], ALU.mult, eng='pool')
                x3 = xn[:].rearrange("p (h d) -> p h d", d=128)
                q3 = qb[:].rearrange("p (h d) -> p h d", d=128)
                cs = cosT[:, tl, :].unsqueeze(1).broadcast_to([128, 4, 16])
                sn = sinT[:, tl, :].unsqueeze(1).broadcast_to([128, 4, 16])
                t3 = sq[:].rearrange("p (h d) -> p h d", d=128)
                k.tt(t3[:, :, 0:16], x3[:, :, 0:16], cs, ALU.mult)
                k.tt(t3[:, :, 16:32], x3[:, :, 16:32], sn, ALU.mult)
                k.tt(q3[:, :, 0:16], t3[:, :, 0:16], t3[:, :, 16:32], ALU.subtract)
                k.tt(t3[:, :, 32:48], x3[:, :, 16:32], cs, ALU.mult)
                k.tt(t3[:, :, 48:64], x3[:, :, 0:16], sn, ALU.mult)
                k.tt(q3[:, :, 16:32], t3[:, :, 32:48], t3[:, :, 48:64], ALU.add)
                k.copy(q3[:, :, 32:128], x3[:, :, 32:128], eng='pool')
                yield
                yield
                yield
                yield
                pb = PS[4 + ecnt[0] % 2]
                ecnt[0] += 1
                pbb = pb[:].bitcast(BF16)
                k.tr([(pbb[:, j * 128:(j + 1) * 128], qb[:, j * 128:(j + 1) * 128]) for j in range(4)], identB[:])
                qT = k.nxt('ip_qT')
                k.copy(qT[:], pbb[:, 0:512], eng='act')
                if which == 0:
                    fpt = tl - HT
                    dst = S['swq'][g][:, h0:h0 + 4, fpt * 128:(fpt + 1) * 128]
                else:
                    dst = S['swk'][g][:, h0:h0 + 4, tl * 128:(tl + 1) * 128]
                k.dma(dst, qT[:].rearrange("p (h t) -> p h t", t=128))

            def epi_v(tag, c0, ncols, t0, nt, o):
                g, which, h0 = tag
                tl = (t_base + t0) // 128
                qb = k.nxt('ip_qb')
                if own:
                    k.copy(qb[:], o, eng='act')
                else:
                    k.act(qb[:], o, AF.Copy, scale=flag[:, 0:1])
                k.dma(S['swv'][g][tl][:, h0:h0 + 4, :], qb[:].rearrange("p (h d) -> p h d", d=128))
                return None
                yield

            tiles_all = [(i * 128, 128) for i in range(TPB)]
            tiles_fp = ([(-128, 128)] if first_own else []) + tiles_all
            for g, (w, dl) in enumerate(cfg.GROUPS):
                gbase = cfg.c_sw + g * 3 * SWW
                need_kv = (lb + 1) * TB > cfg.FP0 - w
                if need_kv:
                    jobs.append(dict(kind='T', w=w_in, krows=D, cr=[(gbase + SWW + h0 * 128, 512, (g, 1, h0)) for h0 in range(0, HS, 4)],
                             act_fn=act_fn, toks=tiles_all, epi=epi_qk, banks=[PS[0], PS[1], PS[2], PS[3]]))
                    jobs.append(dict(kind='T', w=w_in, krows=D, cr=[(gbase + 2 * SWW + h0 * 128, 512, (g, 2, h0)) for h0 in range(0, HS, 4)],
                             act_fn=act_fn, toks=tiles_all, epi=epi_v, banks=[PS[0], PS[1], PS[2], PS[3]]))
                if own:
                    jobs.append(dict(kind='T', w=w_in, krows=D, cr=[(gbase + h0 * 128, 512, (g, 0, h0)) for h0 in range(0, HS, 4)],
                             act_fn=act_fn, toks=tiles_fp, epi=epi_qk, banks=[PS[0], PS[1], PS[2], PS[3]]))

            if own:
                def epi_z(tag, c0, ncols, t0, nt, o):
                    fpt = (t_base + t0) // 128 - HT
                    zt = k.nxt('ip_t512')
                    k.act(zt[:, 0:ncols], o, AF.Silu)
                    k.dma(S['zs'][fpt][:, c0 - cfg.c_z:c0 - cfg.c_z + ncols], zt[:, 0:ncols])
                    return None
                    yield
                jobs.append(dict(kind='T', w=w_in, krows=D, cr=[(cfg.c_z + c0, min(512, DNW - c0), 0) for c0 in range(0, DNW, 512)],
                             act_fn=act_fn, toks=tiles_fp, epi=epi_z, banks=[PS[0], PS[1], PS[2], PS[3]]))

                def epi_g(tag, c0, ncols, t0, nt, o):
                    ch = (c0 - cfg.c_gate) // 128
                    ob = k.nxt('ip_ob')
                    k.act(ob[0:ncols, 0:nt], o, AF.Sigmoid, bias=bgT[0:ncols, ch:ch + 1], scale=1.0)
                    f0 = (t_base + t0) - cfg.FP0
                    k.dma(S['gT'][ch][:, f0:f0 + nt], ob[:, 0:nt])
                    return None
                    yield
                jobs.append(dict(kind='F', w=w_in, krows=D, cr=[(cfg.c_gate + c0, 512, 0) for c0 in range(0, 2 * D, 512)],
                             act_fn=act_fn, toks=([(-128, 128)] if first_own else []) + [(t_, 512) for t_ in range(0, TB, 512)], epi=epi_g, banks=[PS[0], PS[1], PS[2], PS[3]]))
            jobs.append(ba_job)
            run_jobs(k, jobs)
        k.s.barrier()
        k.cur = None
    _stage_dn(nc, k, cfg, L)


def _stage_dn(nc, k, cfg, L):
    HA, TB = cfg.HA, cfg.TB
    V = k.cfgvals
    NT, TPB, HT, NFPT, NFPTOK = V['NT'], V['TPBI'], V['HT'], V['NFPT'], V['NFPTOK']
    S, cin = k.S, L['cin']
    identB, identF, onesF = L['identB'], L['identF'], L['onesF']
    NQC = 3 * HA
    DNW = cfg.DNW
    S['oaT'] = k.dram("oaT", [128, HA, NFPTOK], BF16)
    import os as _os
    GH = min(int(_os.environ.get('DN_GH', '8')), HA)
    with contextlib.ExitStack() as sx:
        k.cur = sx
        sbt = lambda name, shape, dt: sx.enter_context(nc.sbuf_tensor(name, list(shape), dt))
        nm_incl = sbt("dn_nmincl", [128, 128], F32)
        m_sneg = sbt("dn_msneg", [128, 128], F32)
        tri = sbt("dn_tri", [128, 128], F32)
        ogain = sbt("dn_ogain", [128, 128], F32)
        k.dma(nm_incl[:], cin['nm_incl'])
        k.dma(m_sneg[:], cin['m_sneg'])
        k.dma(tri[:], cin['tri_incl'])
        k.dma(ogain[:], L['ogain_in'])
        binv = sbt("dn_binv", [128, 7 * 128], F32)
        k.dma(binv[:], cin['binv'])
        Sf = [sbt("dn_Sf%d" % h, [128, 128], F32) for h in range(HA)]
        Sb = [sbt("dn_Sb%d" % h, [128, 128], BF16) for h in range(HA)]
        for h in range(HA):
            k.memset(Sf[h][:], 0.0, eng='pool')
            k.memset(Sb[h][:], 0.0, eng='pool')
        k.ring('dn_q', [128, NQC, 128], BF16, 2)
        k.ring('dn_bg', [128, 2 * HA], F32, 2)
        k.ring('dn_sc', [128, 6, HA], F32, 2)
        k.ring('dn_z', [128, DNW], F32, 2)
        k.ring('dn_oa', [128, HA, 128], BF16, 2)
        k.ring('dn_oaT', [128, HA, 128], BF16, 2)
        R = GH + 1
        k.ring('dn_tok', [128, 7, 128], BF16, R)
        k.ring('dn_ft', [128, 4, 128], BF16, R)
        k.ring('dn_sm', [128, 16], F32, R)
        k.ring('dn_gbc', [128, 128], F32, R)
        k.ring('dn_arg', [128, 128], F32, GH)
        k.ring('dn_gam', [128, 128], F32, R)
        k.ring('dn_gsn', [128, 128], F32, GH)
        k.ring('dn_xn', [128, 2, 128], F32, 2 * GH + 2)
        k.ring('dn_xoff', [128, 2, 128], F32, 3 * GH)
        k.ring('dn_pf', [128, 128], F32, 4 * GH)
        k.ring('dn_pb', [128, 128], BF16, R)
        k.ring('dn_u', [128, 128], F32, R)
        k.ring('dn_wT', [128, 128], BF16, R)
        k.ring('dn_aqk', [128, 128], BF16, R)
        k.ring('dn_vn', [128, 128], BF16, R)
        k.ring('dn_ot', [128, 128], F32, R)
        k.ring('dn_junk', [128, 128], BF16, 4)
        PS = k.ps
        qi = [0]

        def psq(n=1):
            q = qi[0] % 8
            qi[0] += 1
            return PS[q][:, 0:n * 128]

        for c in range(NT):
            fp = c >= HT
            fpt = c - HT
            lb, off = c // TPB, (c % TPB) * 128
            Q = k.nxt('dn_q')
            bgt = k.nxt('dn_bg')
            sc = k.nxt('dn_sc')
            k.dma(Q[:], S['dnq'][lb][:, :, off:off + 128])
            k.dma(bgt[:], S['bg'][c])
            if fp:
                zt = k.nxt('dn_z')
                k.dma(zt[:], S['zs'][fpt])
                oa = k.nxt('dn_oa')
            pg = psq(1)
            k.mm(pg[:, 0:HA], [(tri[:], bgt[:, HA:2 * HA])])
            k.mm(pg[:, HA:2 * HA], [(onesF[:], bgt[:, HA:2 * HA])])
            k.copy(sc[:, 0, :], bgt[:, 0:HA], eng='pool')
            k.copy(sc[:, 4, :], pg[:, 0:HA])
            k.act(sc[:, 3, :], pg[:, 0:HA], AF.Exp)
            k.act(sc[:, 5, :], pg[:, HA:2 * HA], AF.Exp)
            k.tt(sc[:, 2, :], pg[:, HA:2 * HA], sc[:, 4, :], ALU.subtract)
            k.act(sc[:, 2, :], sc[:, 2, :], AF.Exp)
            k.tt(sc[:, 1, :], sc[:, 0, :], sc[:, 3, :], ALU.mult, eng='pool')

            for h0 in range(0, HA, GH):
                hs = list(range(h0, min(HA, h0 + GH)))
                T = {}
                for h in hs:
                    d = T[h] = {}
                    p1 = psq(2).bitcast(BF16)
                    k.tr([(p1[:, 0:128], Q[:, HA + h, :]), (p1[:, 128:256], Q[:, h, :]),
                          (p1[:, 256:384], Q[:, 2 * HA + h, :])], identB[:])
                    d['p1'] = p1
                for h in hs:
                    d = T[h]
                    p1 = d['p1']
                    sm = d['sm'] = k.nxt('dn_sm')
                    j1, j2 = k.nxt('dn_junk'), k.nxt('dn_junk')
                    k.act(j1[:], p1[:, 0:128], AF.Square, accum=sm[:, 0:1])
                    k.act(j2[:], p1[:, 128:256], AF.Square, accum=sm[:, 1:2])
                    k.act(sm[:, 2:4], sm[:, 0:2], AF.Sqrt, bias=EPS, scale=1.0)
                    k.recip(sm[:, 4:6], sm[:, 2:4])
                    k.ts(sm[:, 6:9], sc[:, 0:3, h], sm[:, 4:5], None, op0=ALU.mult)
                    k.ts(sm[:, 9:10], sm[:, 5:6], 128.0 ** -0.5, None, op0=ALU.mult)
                    k.tt(sm[:, 10:11], sm[:, 9:10], sc[:, 3, h:h + 1], ALU.mult)
                for h in hs:
                    d = T[h]
                    p1, sm = d['p1'], d['sm']
                    tk = d['tk'] = k.nxt('dn_tok')
                    k.ts(tk[:, 1, :], p1[:, 0:128], sm[:, 6:7], None, op0=ALU.mult)
                    k.ts(tk[:, 3, :], p1[:, 0:128], sm[:, 8:9], None, op0=ALU.mult)
                    k.ts(tk[:, 6, :], p1[:, 256:384], sc[:, 0, h:h + 1], None, op0=ALU.mult)
                    if fp:
                        k.ts(tk[:, 5, :], p1[:, 128:256], sm[:, 10:11], None, op0=ALU.mult)
                    k.act(tk[:, 0, :], p1[:, 0:128], AF.Copy, scale=sm[:, 4:5])
                    k.act(tk[:, 2, :], p1[:, 0:128], AF.Copy, scale=sm[:, 7:8])
                    if fp:
                        k.act(tk[:, 4, :], p1[:, 128:256], AF.Copy, scale=sm[:, 9:10])
                    gbc = d['gbc'] = k.nxt('dn_gbc')
                    k.ts(gbc[:], onesF[:], bgt[:, HA + h:HA + h + 1], None, op0=ALU.mult, eng='pool')
                for h in hs:
                    d = T[h]
                    tk = d['tk']
                    p2 = psq(2).bitcast(BF16)
                    nb = 4 if fp else 2
                    srcs = [tk[:, 0, :], tk[:, 1, :], tk[:, 4, :], tk[:, 5, :]][:nb]
                    k.tr([(p2[:, j * 128:(j + 1) * 128], srcs[j]) for j in range(nb)], identB[:])
                    ft = d['ft'] = k.nxt('dn_ft')
                    k.copy(ft[:, 0:nb, :], p2[:, 0:nb * 128].rearrange("p (a b) -> p a b", b=128), eng='act')
                    pG = psq(1)
                    k.mm(pG, [(d['gbc'][:], tri[:])])
                    arg = k.nxt('dn_arg')
                    k.stt(arg[:], pG, sc[:, 4, h:h + 1], nm_incl[:], ALU.subtract, ALU.add)
                    gam = d['gam'] = k.nxt('dn_gam')
                    k.act(gam[:], arg[:], AF.Exp)
                    gsn = d['gsn'] = k.nxt('dn_gsn')
                    k.tt(gsn[:], gam[:], m_sneg[:], ALU.mult, eng='pool')
                for h in hs:
                    d = T[h]
                    ft = d['ft']
                    pA = psq(1)
                    k.mm(pA, [(ft[:, 0, :], ft[:, 1, :])])
                    xn = d['xn'] = k.nxt('dn_xn')
                    k.tt(xn[:, 0, :], pA, d['gsn'][:], ALU.mult)
                for h in hs:
                    d = T[h]
                    xn = d['xn']
                    pN = psq(1)
                    k.tr([(pN, xn[:, 0, :])], identF[:])
                    k.copy(xn[:, 1, :], pN, eng='act')
                    xd = d['xd'] = k.nxt('dn_xn')
                    k.tt(xd[:], xn[:], binv[:, 0:128].unsqueeze(1).broadcast_to([128, 2, 128]), ALU.mult, eng='pool')
                    d['off'] = []
                    for li in range(3):
                        xo = k.nxt('dn_xoff')
                        k.tt(xo[:, 0, :], xn[:, 0, :], binv[:, (1 + li) * 128:(2 + li) * 128], ALU.mult, eng='pool')
                        k.tt(xo[:, 1, :], xn[:, 1, :], binv[:, (4 + li) * 128:(5 + li) * 128], ALU.mult, eng='pool')
                        d['off'].append(xo)
                    pf = d['pf'] = k.nxt('dn_pf')
                    k.tt(pf[:], xd[:, 0, :], identF[:], ALU.add)
                for lvl in range(1, 4):
                    for h in hs:
                        d = T[h]
                        xo = d['xd']
                        pX = psq(2)
                        xn2 = k.nxt('dn_xn')
                        if lvl < 3:
                            k.mm_multi([(pX[:, 0:128], [(xo[:, 1, :], xo[:, 0, :])]),
                                        (pX[:, 128:256], [(xo[:, 0, :], xo[:, 1, :])])])
                            k.copy(xn2[:].rearrange("p a b -> p (a b)"), pX, eng=('act' if h % 2 else 'dve'))
                        else:
                            k.mm(pX[:, 128:256], [(xo[:, 0, :], xo[:, 1, :])])
                            k.copy(xn2[:, 1, :], pX[:, 128:256], eng=('act' if h % 2 else 'dve'))
                        d['xd'] = xn2
                    for h in hs:
                        d = T[h]
                        pP = psq(1)
                        k.mm(pP, [(d['xd'][:, 1, :], d['pf'][:])])
                        pf2 = k.nxt('dn_pf')
                        k.tt(pf2[:], pP, d['pf'][:], ALU.add)
                        d['pf'] = pf2
                for li in range(3):
                    for h in hs:
                        d = T[h]
                        pT = psq(1)
                        k.tr([(pT, d['pf'][:])], identF[:])
                        td = d['td'] = k.nxt('dn_pf')
                        k.copy(td[:], pT, eng='act')
                        pM = psq(1)
                        k.mm(pM, [(d['off'][li][:, 1, :], d['pf'][:])])
                        m1 = d['m1'] = k.nxt('dn_pf')
                        k.copy(m1[:], pM, eng=('dve' if h % 2 else 'act'))
                    for h in hs:
                        d = T[h]
                        pM2 = psq(1)
                        k.mm(pM2, [(d['td'][:], d['m1'][:])])
                        pf2 = k.nxt('dn_pf')
                        k.tt(pf2[:], pM2, d['pf'][:], ALU.add)
                        d['pf'] = pf2
                for h in hs:
                    d = T[h]
                    pb = d['pb'] = k.nxt('dn_pb')
                    k.copy(pb[:], d['pf'][:], eng='pool')
                for h in hs:
                    d = T[h]
                    tk, ft, pb = d['tk'], d['ft'], d['pb']
                    pU = psq(2)
                    k.mm_multi([(pU[:, 0:128], [(pb[:], tk[:, 6, :])]),
                                (pU[:, 128:256], [(tk[:, 2, :], pb[:])])])
                    u = d['u'] = k.nxt('dn_u')
                    wT = d['wT'] = k.nxt('dn_wT')
                    k.copy(u[:], pU[:, 0:128], eng=('act' if h % 2 else 'dve'))
                    k.copy(wT[:], pU[:, 128:256], eng=('act' if h % 2 else 'dve'))
                    if fp:
                        pQ = psq(1)
                        k.mm(pQ, [(ft[:, 0, :], ft[:, 2, :])])
                        aqk = d['aqk'] = k.nxt('dn_aqk')
                        k.tt(aqk[:], pQ, d['gam'][:], ALU.mult)
                for h in hs:
                    d = T[h]
                    pV = psq(1)
                    k.mm(pV, [(d['wT'][:], Sb[h][:])])
                    vn = d['vn'] = k.nxt('dn_vn')
                    k.tt(vn[:], d['u'][:], pV, ALU.subtract)
                for s0 in range(0, len(hs), 4):
                    sub = hs[s0:s0 + 4]
                    for h in sub:
                        d = T[h]
                        if fp:
                            pO = d['pO'] = psq(1)
                            k.mm(pO, [(d['ft'][:, 3, :], Sb[h][:]), (d['aqk'][:], d['vn'][:])])
                        pS = psq(1)
                        k.mm(pS, [(d['tk'][:, 3, :], d['vn'][:])])
                        k.stt(Sf[h][:], Sf[h][:], sc[:, 5, h:h + 1], pS, ALU.mult, ALU.add)
                        k.copy(Sb[h][:], Sf[h][:], eng='act')
                    if fp:
                        for h in sub:
                            d = T[h]
                            sm, pO = d['sm'], d['pO']
                            jk = k.nxt('dn_junk')
                            k.act(jk[:], pO, AF.Square, accum=sm[:, 11:12])
                            k.act(sm[:, 12:13], sm[:, 11:12], AF.Sqrt, bias=EPS, scale=1.0 / 128)
                            k.recip(sm[:, 13:14], sm[:, 12:13])
                            ot = k.nxt('dn_ot')
                            k.stt(ot[:], pO, sm[:, 13:14], ogain[:], ALU.mult, ALU.mult)
                            k.tt(oa[:, h, :], ot[:], zt[:, h * 128:(h + 1) * 128], ALU.mult, eng='pool')
            if fp:
                oaTs = k.nxt('dn_oaT')
                for h0 in range(0, HA, 4):
                    pT = psq(2).bitcast(BF16)
                    nb = min(4, HA - h0)
                    k.tr([(pT[:, j * 128:(j + 1) * 128], oa[:, h0 + j, :]) for j in range(nb)], identB[:])
                    k.copy(oaTs[:, h0:h0 + nb, :], pT[:, 0:nb * 128].rearrange("p (a b) -> p a b", b=128), eng='act')
                k.dma(S['oaT'][:, :, fpt * 128:(fpt + 1) * 128], oaTs[:])
        k.s.barrier()
        k.cur = None
    _stage_attn(nc, k, cfg, L)


def _stage_attn(nc, k, cfg, L):
    HS, NG, NTOK, PRE = cfg.HS, cfg.NG, cfg.NTOK, cfg.PRE
    V = k.cfgvals
    NT, HT, NFPT, NFPTOK, NMASK = V['NT'], V['HT'], V['NFPT'], V['NFPTOK'], V['NMASK']
    S, cin, flag, identB = k.S, L['cin'], L['flag'], L['identB']
    S['obT'] = k.dram("obT", [128, HS, NFPTOK], BF16)
    PS = k.ps
    with contextlib.ExitStack() as sx:
        k.cur = sx
        sbt = lambda name, shape, dt: sx.enter_context(nc.sbuf_tensor(name, list(shape), dt))
        amask = sbt("at_mask", [128, NMASK * 128], F32)
        k.dma(amask[:], cin['amask'])
        KT = [sbt("at_K%d" % i, [128, NG, NTOK], BF16) for i in range(2)]
        QT = [sbt("at_Q%d" % i, [128, NG, NFPTOK], BF16) for i in range(2)]
        VT = [sbt("at_V%d" % i, [128, NG, NT, 132], BF16) for i in range(2)]
        for i in range(2):
            for g_ in range(NG):
                k.copy(VT[i][:, g_, :, 128:130], L['onesB'][:, 0:2 * NT].rearrange("p (a b) -> p a b", b=2))
        k.ring('at_e', [128, 512], F32, 4)
        k.ring('at_pm', [128, 512], BF16, 4)
        k.ring('at_sm', [128, 4], F32, 2)
        k.ring('at_ob', [128, 128], BF16, 2)
        k.ring('at_obT', [128, NFPTOK], BF16, 2)
        sbank = [0]
        scale = 128.0 ** -0.5
        for h in range(HS):
            Kt, Qt, Vt = KT[h % 2], QT[h % 2], VT[h % 2]
            for g in range(NG):
                k.dma(Kt[:, g, :], S['swk'][g][:, h, :])
                k.dma(Qt[:, g, :], S['swq'][g][:, h, :])
                k.dma(Vt[:, g, :, 0:128], S['swv'][g][:, :, h, :].rearrange("t p d -> p t d"))
            obT = k.nxt('at_obT')
            for f in range(NFPT):
                qt = HT + f
                batches = []
                for g, (w, dl) in enumerate(cfg.GROUPS):
                    js = [j for j in range(dl + 1) if qt - j >= 0]
                    for cls in (0, 1):
                        sel = [j for j in js if (((qt - j) * 128 < PRE) == bool(cls))]
                        for i0 in range(0, len(sel), 4):
                            batches.append((g, cls, sel[i0:i0 + 4]))
                npv = sum(len(b_[2]) for b_ in batches)
                pO = PS[4 + f % 2][:, 0:130]
                pvc = [0]

                def stage_a(g, cls, js):
                    n = len(js)
                    pS = PS[sbank[0] % 4][:, 0:n * 128]
                    sbank[0] += 1
                    k.mm_multi([(pS[:, i * 128:(i + 1) * 128],
                                 [(Kt[:, g, (qt - j) * 128:(qt - j + 1) * 128], Qt[:, g, f * 128:(f + 1) * 128])])
                                for i, j in enumerate(js)])
                    e = k.nxt('at_e')
                    if cls:
                        k.act(e[:, 0:n * 128], pS, AF.Exp, bias=flag[:, 1:2], scale=scale)
                    else:
                        k.act(e[:, 0:n * 128], pS, AF.Exp, scale=scale)
                    pm = k.nxt('at_pm')
                    mi = cfg.mask_idx[(g, js[0])]
                    assert all(cfg.mask_idx[(g, j)] == mi + i for i, j in enumerate(js))
                    k.tt(pm[:, 0:n * 128], e[:, 0:n * 128], amask[:, mi * 128:(mi + n) * 128], ALU.mult,
                         eng=('dve' if sbank[0] % 2 else 'pool'))
                    return pm, [Vt[:, g, qt - j, 0:130] for j in js]

                def stage_b(pm, vvs):
                    fns, reads = [], []
                    for i, vv in enumerate(vvs):
                        first, last = (pvc[0] == 0), (pvc[0] == npv - 1)
                        pvc[0] += 1
                        l = pm[:, i * 128:(i + 1) * 128]
                        fns.append(lambda e_, l=l, vv=vv, first=first, last=last, pO=pO:
                                   e_.matmul(pO, l, vv, start=first, stop=last))
                        reads += [l, vv]
                    k.s.op('pe', fns, k._aps(*reads), k._aps(pO))

                prev = []
                for (g, cls, js) in batches:
                    prev.append(stage_a(g, cls, js))
                    if len(prev) > 2:
                        pm0, v0 = prev.pop(0)
                        stage_b(pm0, v0)
                for pm0, v0 in prev:
                    stage_b(pm0, v0)
                sm = k.nxt('at_sm')
                if h == 0 and f == 1 and 'dbg_pO' in k.debug:
                    dbt = sbt("at_dbg", [128, 132], F32)
                    k.copy(dbt[:, 0:130], pO)
                    k.dump('dbg_pO', dbt[:])
                    dbv = sbt("at_dbgv", [128, 132], F32)
                    k.copy(dbv[:], Vt[:, 0, qt, :])
                    k.dump('dbg_V', dbv[:])
                k.ts(sm[:, 0:1], pO[:, 128:129], 1e-30, None, op0=ALU.add)
                k.recip(sm[:, 1:2], sm[:, 0:1])
                ob = k.nxt('at_ob')
                k.ts(ob[:], pO[:, 0:128], sm[:, 1:2], None, op0=ALU.mult)
                pT = PS[6 + f % 2][:].bitcast(BF16)
                k.tr([(pT[:, 0:128], ob[:])], identB[:])
                k.copy(obT[:, f * 128:(f + 1) * 128], pT[:, 0:128], eng='act')
            k.dma(S['obT'][:, h, :], obT[:])
        k.s.barrier()
        k.cur = None
    _stage_out(nc, k, cfg, L)


def _stage_out(nc, k, cfg, L):
    D, KC, TB, HA, HS, DFF, FC, PRE = cfg.D, cfg.KC, cfg.TB, cfg.HA, cfg.HS, cfg.DFF, cfg.FC, cfg.PRE
    V = k.cfgvals
    NT, TPB, NOB, HT, NFPT, NFPTOK = V['NT'], V['TPB'], V['NOB'], V['HT'], V['NFPT'], V['NFPTOK']
    S, flag, modT, gate_bc, x_loc = k.S, L['flag'], L['modT'], L['gate_bc'], L['x_loc']
    PS = k.ps
    S['x1'] = k.dram("x1", [NFPT, 128, D], F32)
    blocks = [(0, 128)] + [(128 + ob * TB, TB) for ob in range(NOB)]
    with contextlib.ExitStack() as sx:
        k.cur = sx
        k.ring('wp', [128, PIECE], BF16, 8)
        k.ring('wstage', [128, PIECE], F32, 3)
        sbt = lambda name, shape, dt: sx.enter_context(nc.sbuf_tensor(name, list(shape), dt))
        actA = sbt("o_actA", [128, HA, TB], BF16)
        actB = sbt("o_actB", [128, HS, TB], BF16)
        mT = sbt("o_mT", [128, KC, TB], BF16)
        gmix = sbt("o_gmix", [128, D], F32)
        k.dma(gmix[:], gate_bc[0])
        k.ring('o_g', [128, TB], BF16, 3)
        k.ring('o_t', [128, TB], F32, 2)
        k.ring('o_x', [128, 512], F32, 2)
        k.ring('o_x1', [128, 512], F32, 2)
        for (f0, n) in blocks:
            k.dma(actA[:, :, 0:n], S['oaT'][:, :, f0:f0 + n])
            k.dma(actB[:, :, 0:n], S['obT'][:, :, f0:f0 + n])

            def epi_a(tag, c0, ncols, t0, nt, o):
                ch = c0 // 128
                g = k.nxt('o_g')
                k.dma(g[:, 0:nt], S['gT'][ch][:, f0:f0 + nt])
                k.tt(mT[:, ch, 0:nt], o, g[:, 0:nt], ALU.mult)
                return None
                yield

            def epi_b(tag, c0, ncols, t0, nt, o):
                ch = c0 // 128
                g = k.nxt('o_g')
                t = k.nxt('o_t')
                k.dma(g[:, 0:nt], S['gT'][KC + ch][:, f0:f0 + nt])
                k.tt(t[:, 0:nt], o, g[:, 0:nt], ALU.mult)
                k.tt(mT[:, ch, 0:nt], mT[:, ch, 0:nt], t[:, 0:nt], ALU.add, eng='pool')
                return None
                yield

            cr = [(c0, min(512, D - c0), 0) for c0 in range(0, D, 512)]
            jobs = [dict(kind='F', w=L['w_a'], krows=cfg.DNW, cr=cr, act_fn=lambda kc, t0, nt: actA[:, kc, t0:t0 + nt],
                         toks=[(0, n)], epi=epi_a, banks=[PS[0], PS[1], PS[4], PS[5]]),
                    dict(kind='F', w=L['w_b'], krows=cfg.SWW, cr=cr, act_fn=lambda kc, t0, nt: actB[:, kc, t0:t0 + nt],
                         toks=[(0, n)], epi=epi_b, banks=[PS[0], PS[1], PS[4], PS[5]])]

            def epi_o(tag, c0, ncols, t0, nt, o):
                fpt = (f0 + t0) // 128
                lt = HT + fpt
                xt = k.nxt('o_x')
                x1t = k.nxt('o_x1')
                k.dma(xt[:, 0:ncols], x_loc[lt * 128:(lt + 1) * 128, c0:c0 + ncols])
                k.tt(x1t[:, 0:ncols], o, gmix[:, c0:c0 + ncols], ALU.mult)
                k.tt(x1t[:, 0:ncols], x1t[:, 0:ncols], xt[:, 0:ncols], ALU.add, eng='pool')
                k.dma(S['x1'][fpt][:, c0:c0 + ncols], x1t[:, 0:ncols])
                return None
                yield
            jobs.append(dict(kind='T', w=L['w_o'], krows=D, cr=cr, act_fn=lambda kc, t0, nt: mT[:, kc, t0:t0 + nt],
                             toks=[(i * 128, 128) for i in range(n // 128)], epi=epi_o, banks=[PS[2], PS[3], PS[6], PS[7]]))
            run_jobs(k, jobs)
        k.s.barrier()
        k.cur = None

    S['h2h'] = k.dram("h2T_halo", [128, KC, 128], BF16)
    TBU = cfg.TBI
    TPBU = TBU // 128
    NUB = cfg.OWN // TBU
    S['h2'] = [k.dram("h2T_b%d" % b, [128, KC, TBU], BF16) for b in range(NUB)]

    def dst2(t):
        if t == 0:
            return S['h2h']
        return S['h2'][(t - 1) // TPBU][:, :, ((t - 1) % TPBU) * 128:((t - 1) % TPBU + 1) * 128]
    L['norm_stage']("n2", list(range(NFPT)), lambda t: S['x1'][t], modT[:, 4 * KC:5 * KC], modT[:, 3 * KC:4 * KC], dst2)

    S['ffa'] = [k.dram("ffa_b%d" % b, [128, FC, TB], BF16) for b in range(NOB)]
    with contextlib.ExitStack() as sx:
        k.cur = sx
        k.ring('wp', [128, PIECE], BF16, 8)
        k.ring('wstage', [128, PIECE], F32, 2)
        sbt = lambda name, shape, dt: sx.enter_context(nc.sbuf_tensor(name, list(shape), dt))
        actb = sbt("f_act", [128, KC, TBU], BF16)
        acth = sbt("f_acth", [128, KC, 128], BF16)
        convw = sbt("f_convw", [128, 2 * FC, 3], F32)
        ffhalo = sbt("f_halo", [128, 2 * FC, 2], F32)
        gbuf = sbt("f_gbuf", [128, 4, TBU], F32)
        k.dma(convw[:].rearrange("p a b -> p (a b)"), L['convf_in'])
        k.dma(acth[:], S['h2h'])
        k.ring('f_xr', [128, 512 + 2], F32, 2)
        k.ring('f_acc', [128, 512], F32, 2)
        k.ring('f_ob', [128, 512], BF16, 3)
        for ob in range(NUB):
            k.dma(actb[:], S['h2'][ob])

            def act_fn(kc, t0, nt):
                if t0 < 0:
                    return acth[:, kc, 128 + t0:128 + t0 + nt]
                return actb[:, kc, t0:t0 + nt]

            def epi_up(tag, c0, ncols, t0, nt, o):
                c = c0 // 128
                if t0 < 0:
                    k.act(ffhalo[0:ncols, c, :], o, AF.Copy, scale=flag[0:ncols, 0:1])
                    return None
                xr = k.nxt('f_xr')
                acc = k.nxt('f_acc')
                k.copy(xr[:, 0:2], ffhalo[:, c, :], eng='pool')
                k.copy(xr[:, 2:2 + nt], o, eng='act')
                k.copy(ffhalo[:, c, :], xr[:, nt:nt + 2], eng='pool')
                k.ts(acc[:, 0:nt], xr[:, 0:nt], convw[:, c, 0:1], None, op0=ALU.mult)
                for j in (1, 2):
                    k.stt(acc[:, 0:nt], xr[:, j:j + nt], convw[:, c, j:j + 1], acc[:, 0:nt], ALU.mult, ALU.add)
                if tag[0] == 'g':
                    k.act(gbuf[:, (c - tag[1]), t0:t0 + nt], acc[:, 0:nt], AF.Silu)
                else:
                    obt = k.nxt('f_ob')
                    cc = c - FC
                    k.tt(obt[:, 0:nt], acc[:, 0:nt], gbuf[:, cc - tag[1], t0:t0 + nt], ALU.mult, eng='pool')
                    tok_ = ob * TBU + t0
                    k.dma(S['ffa'][tok_ // TB][:, cc, tok_ % TB:tok_ % TB + nt], obt[:, 0:nt])
                return None
                yield
            cr = []
            for c0 in range(0, DFF, 512):
                w_ = min(512, DFF - c0)
                cr.append((c0, w_, ('g', c0 // 128)))
                cr.append((DFF + c0, w_, ('v', c0 // 128)))
            gemm(k, 'F', L['w_up'], D, cr, act_fn, ([(-2, 2)] if ob == 0 else []) + [(t_, 512) for t_ in range(0, TBU, 512)], epi_up, [PS[0], PS[1], PS[2], PS[3]])
        k.s.barrier()
        k.cur = None

    with contextlib.ExitStack() as sx:
        k.cur = sx
        k.ring('wp', [128, PIECE], BF16, 5)
        k.ring('wstage', [128, PIECE], F32, 2)
        sbt = lambda name, shape, dt: sx.enter_context(nc.sbuf_tensor(name, list(shape), dt))
        actf = sbt("d_act", [128, FC, TB], BF16)
        gffn = sbt("d_gffn", [128, D], F32)
        k.dma(gffn[:], gate_bc[1])
        k.ring('d_x', [128, 512], F32, 2)
        k.ring('d_y', [128, 512], F32, 2)
        for ob in range(NOB):
            k.dma(actf[:], S['ffa'][ob])

            def epi_dn(tag, c0, ncols, t0, nt, o):
                own_t = ob * TPB + t0 // 128
                fpt = own_t + 1
                xt = k.nxt('d_x')
                yt = k.nxt('d_y')
                k.dma(xt[:, 0:ncols], S['x1'][fpt][:, c0:c0 + ncols])
                k.tt(yt[:, 0:ncols], o, gffn[:, c0:c0 + ncols], ALU.mult)
                k.tt(yt[:, 0:ncols], yt[:, 0:ncols], xt[:, 0:ncols], ALU.add, eng='pool')
                k.dma(L['out'][own_t * 128:(own_t + 1) * 128, c0:c0 + ncols], yt[:, 0:ncols])
                return None
                yield
            gemm(k, 'T', L['w_down'], DFF, [(c0, min(512, D - c0), 0) for c0 in range(0, D, 512)],
                 lambda kc, t0, nt: actf[:, kc, t0:t0 + nt], [(i * 128, 128) for i in range(TPB)], epi_dn,
                 [PS[0], PS[1], PS[2], PS[3]])
        k.s.barrier()
        k.cur = None
    _finish(nc, k, cfg, L)


def _finish(nc, k, cfg, L):
    k.s.final_wait('sp')
    with nc.Block() as block:
        k.s.emit(block)


def make_in_maps(cfg, inputs):
    D, KC, NTOK, PRE, HA, HS, FC = cfg.D, cfg.KC, cfg.NTOK, cfg.PRE, cfg.HA, cfg.HS, cfg.FC
    f = lambda a: np.ascontiguousarray(np.asarray(a, dtype=np.float32))
    x = np.asarray(inputs['x'], dtype=np.float32)
    c = np.asarray(inputs['c'], dtype=np.float32)
    pos = np.asarray(inputs['positions']).astype(np.int32)
    hc = host_consts(cfg)
    shared = {
        'w_ada': f(inputs['w_ada'][0]), 'b_ada': f(inputs['b_ada'][0]).reshape(1, -1),
        'w_in': f(inputs['w_in'][0]),
        'b_gateT': f(np.asarray(inputs['b_gate'][0]).reshape(2 * KC, 128).T),
        'conv_qkvT': f(np.asarray(inputs['conv_qkv'][0]).reshape(4, 3 * HA, 128).transpose(2, 1, 0).reshape(128, -1)),
        'a_log_bc': f(np.broadcast_to(np.asarray(inputs['a_log'][0])[None, :], (128, HA))),
        'dt_bias_bc': f(np.broadcast_to(np.asarray(inputs['dt_bias'][0])[None, :], (128, HA))),
        'o_gain_bc': f(np.broadcast_to(np.asarray(inputs['o_norm_gain'][0])[None, :], (128, 128))),
        'q_gain_bc': f(np.broadcast_to(np.tile(np.asarray(inputs['q_norm_gain'][0]), 4)[None, :], (128, 512))),
        'k_gain_bc': f(np.broadcast_to(np.tile(np.asarray(inputs['k_norm_gain'][0]), 4)[None, :], (128, 512))),
        'w_a_proj': f(inputs['w_a_proj'][0]), 'w_b_proj': f(inputs['w_b_proj'][0]), 'w_o': f(inputs['w_o'][0]),
        'w_up': f(inputs['w_up'][0]),
        'conv_ffnT': f(np.asarray(inputs['conv_ffn'][0]).reshape(3, 2 * FC, 128).transpose(2, 1, 0).reshape(128, -1)),
        'w_down': f(inputs['w_down'][0]),
    }
    for n, a in hc.items():
        shared['c_' + n] = f(a)
    maps = []
    for core in range(2 * cfg.B):
        b, half = core // 2, core % 2
        m = dict(shared)
        if half == 1:
            xl = x[b, 0:NTOK]
            pl = pos[b, 0:NTOK]
        else:
            xl = np.concatenate([np.zeros((PRE, D), np.float32), x[b, 0:NTOK - PRE]], 0)
            pl = np.concatenate([np.zeros((PRE,), np.int32), pos[b, 0:NTOK - PRE]], 0)
        m['x_loc'] = np.ascontiguousarray(xl)
        m['cT'] = np.ascontiguousarray(c[b].reshape(KC, 128).T)
        m['posT'] = np.ascontiguousarray(pl.reshape(NTOK // 128, 128).T)
        fl = np.zeros((128, 2), np.float32)
        fl[:, 0] = 1.0 if half == 1 else 0.0
        fl[:, 1] = 0.0 if half == 1 else NEG
        m['flagP'] = fl
        maps.append(m)
    return maps


def run(cfg, inputs, debug=(), trace=False):
    nc = build(cfg, debug)
    maps = make_in_maps(cfg, inputs)
    names = set()
    for alloc in nc.allocations:
        pass
    res = run_bass_kernel_spmd(nc, maps, core_ids=list(range(2 * cfg.B)), trace=trace)
    return res


def kernel(**inputs):
    cfg = Cfg()
    res = run(cfg, inputs)
    OWN = cfg.OWN
    outp = np.zeros((cfg.B, 2 * OWN, cfg.D), np.float32)
    for core in range(2 * cfg.B):
        b, half = core // 2, core % 2
        outp[b, half * OWN:(half + 1) * OWN] = np.asarray(res.results[core]['out'])
    return outp
```

```python
import contextlib
import math
import numpy as np
import concourse.bass as bass
import concourse.mybir as mybir
from concourse.bass_utils import run_bass_kernel_spmd

F32 = mybir.dt.float32
BF16 = mybir.dt.bfloat16
I32 = mybir.dt.int32
AF = mybir.ActivationFunctionType
ALU = mybir.AluOpType
AX = mybir.AxisListType


class Cfg:
    def __init__(self, D=4096, NTOK=4096, PRE=2048, HA=16, HS=8, GROUPS=((128, 1), (512, 4), (2048, 16)),
                 DFF=11008, B=4, TB=512, TBI=1024):
        self.D, self.NTOK, self.PRE, self.HA, self.HS, self.GROUPS, self.DFF, self.B, self.TB = \
            D, NTOK, PRE, HA, HS, GROUPS, DFF, B, TB
        self.TBI = TBI
        self.KC = D // 128
        self.NG = len(GROUPS)
        self.DNW = HA * 128
        self.SWW = HS * 128
        self.FC = DFF // 128
        self.OWN = NTOK - PRE
        self.FP0 = PRE - 128
        self.c_qkv = 0
        self.c_z = 3 * self.DNW
        self.c_ba = self.c_z + self.DNW
        self.c_sw = self.c_ba + 2 * HA
        self.c_gate = self.c_sw + 3 * self.NG * self.SWW
        self.INW = self.c_gate + 2 * D
        for (w, d) in GROUPS:
            assert w // d == 128


class Sched:
    ENG = ('pe', 'act', 'dve', 'pool', 'sp')

    def __init__(self, nc, st, nds=8):
        self.nc = nc
        self.prog = {e: [] for e in self.ENG}
        self.sem = {e: st.enter_context(nc.semaphore("sem_" + e)) for e in ('pe', 'act', 'dve', 'pool')}
        self.cnt = {e: 0 for e in ('pe', 'act', 'dve', 'pool')}
        self.nds = nds
        self.dsem = {q: [st.enter_context(nc.semaphore("dsem_%s%d" % (q, i))) for i in range(nds)]
                     for q in ('sp', 'pool', 'act')}
        self.dcnt = {q: 0 for q in ('sp', 'pool', 'act')}
        self.dtok = {q: [None] * nds for q in ('sp', 'pool', 'act')}
        self.seen = {e: {} for e in self.ENG}
        self.bw = {}
        self.br = {}
        self.semobj = {}
        for e, s_ in self.sem.items():
            self.semobj[e] = s_
        for q, l in self.dsem.items():
            for i, s_ in enumerate(l):
                self.semobj[(q, i)] = s_
        self.ninst = {e: 0 for e in self.ENG}

    @staticmethod
    def keys(x):
        if isinstance(x, (str, tuple)):
            return [x]
        return [x.name]

    @staticmethod
    def is_psum(x):
        return (not isinstance(x, (str, tuple))) and 'PSUM' in str(x.space)

    def _deps(self, eng, reads, writes):
        need = {}

        def add(tok):
            if tok is None:
                return
            k, v = tok
            if k == eng and eng == 'pe':
                return
            if self.seen[eng].get(k, 0) >= v:
                return
            if need.get(k, 0) < v:
                need[k] = v
        for r in reads:
            ps_ = self.is_psum(r)
            for kk in self.keys(r):
                add(self.bw.get(kk))
                if ps_:
                    for k, v in self.br.get(kk, {}).items():
                        if k != eng:
                            add((k, v))
        for w in writes:
            for kk in self.keys(w):
                add(self.bw.get(kk))
                for k, v in self.br.get(kk, {}).items():
                    add((k, v))
        for k, v in need.items():
            self.seen[eng][k] = v
        return list(need.items())

    def _commit(self, tok, reads, writes):
        for w in writes:
            for kk in self.keys(w):
                self.bw[kk] = tok
                self.br[kk] = {}
        for r in reads:
            for kk in self.keys(r):
                d = self.br.setdefault(kk, {})
                if d.get(tok[0], 0) < tok[1]:
                    d[tok[0]] = tok[1]

    def op(self, eng, fns, reads, writes):
        if not isinstance(fns, (list, tuple)):
            fns = [fns]
        waits = self._deps(eng, reads, writes)
        self.cnt[eng] += 1
        tok = (eng, self.cnt[eng])
        semobj = self.semobj
        mysem = self.sem[eng]

        def run(e, waits=waits, fns=fns):
            for k, v in waits:
                e.wait_ge(semobj[k], v)
            for f in fns[:-1]:
                f(e)
            fns[-1](e).then_inc(mysem, 1)
        self.prog[eng].append(run)
        self.ninst[eng] += len(fns)
        self.seen[eng][eng] = max(self.seen[eng].get(eng, 0), 0)
        self._commit(tok, reads, writes)
        return tok

    def dma(self, q, out, in_, reads=None, writes=None, **kw):
        reads = [in_] if reads is None else reads
        writes = [out] if writes is None else writes
        i = self.dcnt[q]
        slot = i % self.nds
        waits = self._deps(q, reads, writes)
        prev = self.dtok[q][slot]
        if prev is not None and self.seen[q].get(prev[0], 0) < prev[1]:
            waits.append(prev)
            self.seen[q][prev[0]] = prev[1]
        self.dcnt[q] += 1
        tok = ((q, slot), 16 * (i // self.nds + 1))
        self.dtok[q][slot] = tok
        semobj = self.semobj
        dsem = self.dsem[q][slot]

        def run(e, waits=waits):
            for k, v in waits:
                e.wait_ge(semobj[k], v)
            e.dma_start(out=out, in_=in_, **kw).then_inc(dsem, 16)
        self.prog[q].append(run)
        self.ninst[q] += 1
        self._commit(tok, reads, writes)
        return tok

    def barrier(self):
        toks = {}
        for e, c in self.cnt.items():
            if c > 0:
                toks[e] = c
        for q in self.dtok:
            for tok in self.dtok[q]:
                if tok is not None and toks.get(tok[0], 0) < tok[1]:
                    toks[tok[0]] = tok[1]
        semobj = self.semobj
        for eng in self.ENG:
            items = [(k_, v) for k_, v in toks.items() if self.seen[eng].get(k_, 0) < v and not (k_ == eng and eng == 'pe')]
            for k_, v in items:
                self.seen[eng][k_] = v

            def run(e, items=items):
                for k_, v in items:
                    e.wait_ge(semobj[k_], v)
            self.prog[eng].append(run)

    def final_wait(self, eng='sp'):
        toks = {}
        for tok in list(self.bw.values()):
            if tok is not None and toks.get(tok[0], 0) < tok[1]:
                toks[tok[0]] = tok[1]
        for q in self.dtok:
            for tok in self.dtok[q]:
                if tok is not None and toks.get(tok[0], 0) < tok[1]:
                    toks[tok[0]] = tok[1]
        semobj = self.semobj
        items = list(toks.items())

        def run(e):
            for k, v in items:
                e.wait_ge(semobj[k], v)
        self.prog[eng].append(run)

    def emit(self, block):
        prog = self.prog

        @block.tensor
        def _(e):
            for f in prog['pe']:
                f(e)

        @block.scalar
        def _(e):
            for f in prog['act']:
                f(e)

        @block.vector
        def _(e):
            for f in prog['dve']:
                f(e)

        @block.gpsimd
        def _(e):
            for f in prog['pool']:
                f(e)

        @block.sync
        def _(e):
            for f in prog['sp']:
                f(e)


class K:
    def __init__(self, nc, st, cfg, debug=()):
        self.nc, self.st, self.cfg = nc, st, cfg
        self.s = Sched(nc, st)
        self.debug = set(debug)
        self.rings = {}
        self.ps = [st.enter_context(nc.psum_tensor("psb%d" % i, [128, 512], F32)) for i in range(8)]
        self.uid = 0
        self.castrr = 0
        self.preloaded = {}

    def sb(self, name, shape, dtype):
        stack = self.cur if getattr(self, 'cur', None) is not None else self.st
        return stack.enter_context(self.nc.sbuf_tensor(name, list(shape), dtype))

    def ring(self, name, shape, dtype, n):
        self.uid += 1
        self.rings[name] = [[self.sb("%s_u%d_%d" % (name, self.uid, i), shape, dtype) for i in range(n)], 0]

    def nxt(self, name):
        r = self.rings[name]
        t = r[0][r[1] % len(r[0])]
        r[1] += 1
        return t

    def dram(self, name, shape, dtype):
        kind = "ExternalOutput" if name in self.debug else "Internal"
        return self.nc.dram_tensor(name, list(shape), dtype, kind=kind).ap()

    def dump(self, name, ap, dtype=F32):
        if name in self.debug:
            d = self.nc.dram_tensor(name, list(ap.shape), dtype, kind="ExternalOutput").ap()
            self.dma(d, ap)

    @staticmethod
    def _aps(*xs):
        return [x for x in xs if x is not None and hasattr(x, 'name') and hasattr(x, 'ap')]

    def act(self, out, in_, func, bias=None, scale=None, accum=None, eng='act'):
        kw = {}
        if bias is not None:
            kw['bias'] = bias
        if scale is not None:
            kw['scale'] = scale
        if accum is not None:
            kw['accum_out'] = accum
        return self.s.op('act', lambda e: e.activation(out=out, in_=in_, func=func, **kw),
                         self._aps(in_, bias, scale), self._aps(out, accum))

    def ts(self, out, in0, s1, s2=None, op0=ALU.mult, op1=None, eng='dve', accum=None):
        kw = {}
        if op1 is not None:
            kw['op1'] = op1
        if accum is not None:
            kw['accum_out'] = accum
        return self.s.op(eng, lambda e: e.tensor_scalar(out=out, in0=in0, scalar1=s1, scalar2=s2, op0=op0, **kw),
                         self._aps(in0, s1, s2), self._aps(out, accum))

    def stt(self, out, in0, scalar, in1, op0, op1):
        return self.s.op('dve', lambda e: e.scalar_tensor_tensor(out=out, in0=in0, scalar=scalar, in1=in1,
                                                                  op0=op0, op1=op1),
                         self._aps(in0, scalar, in1), self._aps(out))

    def tt(self, out, in0, in1, op, eng='dve'):
        return self.s.op(eng, lambda e: e.tensor_tensor(out=out, in0=in0, in1=in1, op=op),
                         self._aps(in0, in1), self._aps(out))

    def copy(self, out, in_, eng='dve'):
        if eng == 'act':
            return self.s.op('act', lambda e: e.copy(out=out, in_=in_), self._aps(in_), self._aps(out))
        return self.s.op(eng, lambda e: e.tensor_copy(out=out, in_=in_), self._aps(in_), self._aps(out))

    def memset(self, ap, val, eng='dve'):
        return self.s.op(eng, lambda e: e.memset(ap, val), [], self._aps(ap))

    def recip(self, out, in_):
        return self.s.op('dve', lambda e: e.reciprocal(out=out, in_=in_), self._aps(in_), self._aps(out))

    def reduce(self, out, in_, op=ALU.add, axis=AX.X):
        return self.s.op('dve', lambda e: e.tensor_reduce(out=out, in_=in_, axis=axis, op=op),
                         self._aps(in_), self._aps(out))

    def mm(self, out, pairs, extra_writes=()):
        n = len(pairs)
        fns = []
        reads = []
        for i, (l, r) in enumerate(pairs):
            fns.append(lambda e, l=l, r=r, i=i: e.matmul(out, l, r, start=(i == 0), stop=(i == n - 1)))
            reads += [l, r]
        return self.s.op('pe', fns, self._aps(*reads), self._aps(out))

    def mm_multi(self, groups):
        fns, reads, writes = [], [], []
        for out, pairs in groups:
            n = len(pairs)
            for i, (l, r) in enumerate(pairs):
                fns.append(lambda e, out=out, l=l, r=r, i=i, n=n: e.matmul(out, l, r, start=(i == 0), stop=(i == n - 1)))
                reads += [l, r]
            writes.append(out)
        return self.s.op('pe', fns, self._aps(*reads), self._aps(*writes))

    def tr(self, outs_ins, ident):
        fns, reads, writes = [], [ident], []
        for o, i_ in outs_ins:
            fns.append(lambda e, o=o, i_=i_: e.transpose(o, i_, ident))
            reads.append(i_)
            writes.append(o)
        return self.s.op('pe', fns, self._aps(*reads), self._aps(*writes))

    def dma(self, out, in_, q='sp', **kw):
        return self.s.dma(q, out, in_, **kw)


SLAB_ELEMS = 16384


PIECE = 4096


def gemm(k, kind, w_ap, krows, col_ranges, act_fn, tok_ranges, epi, psum_banks, prefetch_next=None):
    kcn = krows // 128
    assert kcn * 128 == krows
    pend = []
    bank_i = [0]
    npieces_ring = len(k.rings['wp'][0])

    def advance():
        for g in list(pend):
            try:
                next(g)
            except StopIteration:
                pend.remove(g)

    def run_epi(*a):
        advance()
        g = epi(*a)
        if g is not None:
            pend.append(g)
            try:
                next(g)
            except StopIteration:
                pend.remove(g)

    wv = w_ap.rearrange("(kc p) n -> p kc n", p=128)

    def load_piece(c0, ncols, pi):
        key_ = (w_ap.name, c0, ncols, pi)
        if key_ in k.preloaded:
            return k.preloaded.pop(key_)
        pk = max(1, PIECE // ncols)
        ka = pi * pk
        pn = min(pk, kcn - ka)
        stg = k.nxt('wstage')
        stv = stg[:, 0:pn * ncols].rearrange("p (kc n) -> p kc n", n=ncols)
        k.dma(stv, wv[:, ka:ka + pn, c0:c0 + ncols], q='sp')
        pc = k.nxt('wp')
        pv = pc[:, 0:pn * ncols].rearrange("p (kc n) -> p kc n", n=ncols)
        ce = ('dve', 'dve', 'act', 'dve')[k.castrr % 4]
        k.castrr += 1
        k.copy(pv, stv, eng=ce)
        return pv, pn

    items = []
    for r, (c0, ncols, tag) in enumerate(col_ranges):
        assert ncols <= 512
        pk = max(1, PIECE // ncols)
        npc = -(-kcn // pk)
        for pi in range(npc):
            items.append((r, pi))
    npc_max = max(-(-kcn // max(1, PIECE // nc_)) for (_, nc_, _) in col_ranges)
    resident = npc_max * 2 <= npieces_ring
    pf = npc_max if resident else max(1, npieces_ring - 2)
    loaded = {}
    nxt_load = [0]

    def ensure(upto):
        while nxt_load[0] < min(upto, len(items)):
            r_, pi_ = items[nxt_load[0]]
            c0_, ncols_, _ = col_ranges[r_]
            loaded[(r_, pi_)] = load_piece(c0_, ncols_, pi_)
            nxt_load[0] += 1

    pos = 0
    for r, (c0, ncols, tag) in enumerate(col_ranges):
        pk = max(1, PIECE // ncols)
        npc = -(-kcn // pk)
        if r == len(col_ranges) - 1 and prefetch_next is not None:
            ensure(len(items))
            prefetch_next()
        if resident:
            ensure(pos + npc + pf)
            pcs = [loaded.pop((r, pi)) for pi in range(npc)]
            pos += npc

            def wsl(kc, pcs=pcs, pk=pk):
                return pcs[kc // pk][0][:, kc % pk, :]
            if kind == 'T':
                for (t0, nt) in tok_ranges:
                    bank = psum_banks[bank_i[0] % len(psum_banks)]
                    bank_i[0] += 1
                    o = bank[0:nt, 0:ncols]
                    fns, reads = [], []
                    for j in range(kcn):
                        l = act_fn(j, t0, nt)
                        r_ = wsl(j)
                        fns.append(lambda e, o=o, l=l, r_=r_, st_=(j == 0), sp_=(j == kcn - 1):
                                   e.matmul(o, l, r_, start=st_, stop=sp_))
                        reads += [l, r_]
                    k.s.op('pe', fns, k._aps(*reads), k._aps(o))
                    run_epi(tag, c0, ncols, t0, nt, o)
            else:
                for cj in range(0, ncols, 128):
                    cw = min(128, ncols - cj)
                    for (t0, nt) in tok_ranges:
                        bank = psum_banks[bank_i[0] % len(psum_banks)]
                        bank_i[0] += 1
                        o = bank[0:cw, 0:nt]
                        fns, reads = [], []
                        for j in range(kcn):
                            l = wsl(j)[:, cj:cj + cw]
                            r_ = act_fn(j, t0, nt)
                            fns.append(lambda e, o=o, l=l, r_=r_, st_=(j == 0), sp_=(j == kcn - 1):
                                       e.matmul(o, l, r_, start=st_, stop=sp_))
                            reads += [l, r_]
                        k.s.op('pe', fns, k._aps(*reads), k._aps(o))
                        run_epi(tag, c0 + cj, cw, t0, nt, o)
        else:
            assert kind == 'T' and len(tok_ranges) <= len(psum_banks)
            outs = [psum_banks[ti][0:nt, 0:ncols] for ti, (t0, nt) in enumerate(tok_ranges)]
            for pi in range(npc):
                ensure(pos + 1 + pf)
                pv, pn = loaded.pop((r, pi))
                pos += 1
                ka = pi * pk
                for ti, (t0, nt) in enumerate(tok_ranges):
                    o = outs[ti]
                    fns, reads = [], []
                    for j in range(pn):
                        l = act_fn(ka + j, t0, nt)
                        r_ = pv[:, j, :]
                        fns.append(lambda e, o=o, l=l, r_=r_, st_=(pi == 0 and j == 0), sp_=(pi == npc - 1 and j == pn - 1):
                                   e.matmul(o, l, r_, start=st_, stop=sp_))
                        reads += [l, r_]
                    k.s.op('pe', fns, k._aps(*reads), k._aps(o))
                    if pi == npc - 1:
                        run_epi(tag, c0, ncols, t0, nt, o)
    while pend:
        advance()


def run_jobs(k, jobs):
    for i, j in enumerate(jobs):
        nxt = jobs[i + 1] if i + 1 < len(jobs) else None

        def pf(nxt=nxt):
            if nxt is None:
                return
            c0, ncols, _ = nxt['cr'][0]
            kcn = nxt['krows'] // 128
            pk = max(1, PIECE // ncols)
            npc = min(-(-kcn // pk), 4)
            wv = nxt['w'].rearrange("(kc p) n -> p kc n", p=128)
            for pi in range(npc):
                ka = pi * pk
                pn = min(pk, kcn - ka)
                stg = k.nxt('wstage')
                stv = stg[:, 0:pn * ncols].rearrange("p (kc n) -> p kc n", n=ncols)
                k.dma(stv, wv[:, ka:ka + pn, c0:c0 + ncols], q='sp')
                pc = k.nxt('wp')
                pv = pc[:, 0:pn * ncols].rearrange("p (kc n) -> p kc n", n=ncols)
                ce = ('dve', 'dve', 'act', 'dve')[k.castrr % 4]
                k.castrr += 1
                k.copy(pv, stv, eng=ce)
                k.preloaded[(nxt['w'].name, c0, ncols, pi)] = (pv, pn)
        gemm(k, j['kind'], j['w'], j['krows'], j['cr'], j['act_fn'], j['toks'], j['epi'], j['banks'],
             prefetch_next=(pf if nxt is not None else None))


EPS = 1e-6
TWO_PI = 2.0 * math.pi
CW1 = 6.28125
CW2 = TWO_PI - CW1
NEG = -30000.0


def host_consts(cfg):
    c = {}
    c['identF'] = np.eye(128, dtype=np.float32)
    c['onesF'] = np.ones((128, 128), dtype=np.float32)
    ii = np.arange(128)
    c['nm_incl'] = np.where(ii[None, :] >= ii[:, None], 0.0, NEG).astype(np.float32)
    c['m_sneg'] = -(ii[None, :] > ii[:, None]).astype(np.float32)
    c['tri_incl'] = (ii[:, None] <= ii[None, :]).astype(np.float32)
    mm_ = [(ii[:, None] // 16 == ii[None, :] // 16)]
    offs = [((ii[:, None] // b) % 2 == 0) & (ii[None, :] // b == ii[:, None] // b + 1) for b in (16, 32, 64)]
    mm_ += offs + [o.T for o in offs]
    c['binv'] = np.ascontiguousarray(np.concatenate([m.astype(np.float32) for m in mm_], 1))
    half = 16
    inv = (np.float32(500000.0) ** (-(np.arange(half, dtype=np.float32) * np.float32(2.0) / np.float32(32)))).astype(np.float32)
    c['invf'] = np.ascontiguousarray(np.broadcast_to(inv[None, :], (128, half))).astype(np.float32)
    ms = []
    idx = []
    for g, (w, dl) in enumerate(cfg.GROUPS):
        for j in range(dl + 1):
            diff = 128 * j + ii[None, :] - ii[:, None]
            m = (diff >= 0) & (diff % dl == 0) & (diff <= 128 * dl)
            ms.append(m.astype(np.float32))
            idx.append((g, j))
    c['amask'] = np.ascontiguousarray(np.stack(ms, 1).reshape(128, -1)).astype(np.float32)
    cfg.mask_idx = {gj: i for i, gj in enumerate(idx)}
    return c


def build(cfg, debug=()):
    nc = bass.Bass("TRN2", target_bir_lowering=False)
    st = contextlib.ExitStack()
    with st:
        k = K(nc, st, cfg, debug)
        _build_body(nc, st, k, cfg)
    return nc


def _inp(nc, name, shape, dtype=F32):
    return nc.dram_tensor(name, list(shape), dtype, kind="ExternalInput").ap()


def _build_body(nc, st, k, cfg):
    D, KC, NTOK, PRE, TB, HA, HS, NG, DFF, FC = cfg.D, cfg.KC, cfg.NTOK, cfg.PRE, cfg.TB, cfg.HA, cfg.HS, cfg.NG, cfg.DFF, cfg.FC
    NT = NTOK // 128
    TBI = cfg.TBI
    NLB = NTOK // TBI
    TPB = TB // 128
    TPBI = TBI // 128
    OWN = cfg.OWN
    NOB = OWN // TB
    LB0 = PRE // TBI
    HT = PRE // 128 - 1
    NFPT = OWN // 128 + 1
    NFPTOK = NFPT * 128
    DNW, SWW = cfg.DNW, cfg.SWW
    hc = host_consts(cfg)
    NMASK = hc['amask'].shape[1] // 128

    x_loc = _inp(nc, "x_loc", [NTOK, D])
    cT_in = _inp(nc, "cT", [128, KC])
    pos_in = _inp(nc, "posT", [128, NT], I32)
    flag_in = _inp(nc, "flagP", [128, 2])
    w_ada = _inp(nc, "w_ada", [D, 6 * D])
    b_ada = _inp(nc, "b_ada", [1, 6 * D])
    w_in = _inp(nc, "w_in", [D, cfg.INW])
    bgate_in = _inp(nc, "b_gateT", [128, 2 * KC])
    convq_in = _inp(nc, "conv_qkvT", [128, 3 * HA * 4])
    alog_in = _inp(nc, "a_log_bc", [128, HA])
    dtb_in = _inp(nc, "dt_bias_bc", [128, HA])
    ogain_in = _inp(nc, "o_gain_bc", [128, 128])
    qgain_in = _inp(nc, "q_gain_bc", [128, 512])
    kgain_in = _inp(nc, "k_gain_bc", [128, 512])
    w_a = _inp(nc, "w_a_proj", [DNW, D])
    w_b = _inp(nc, "w_b_proj", [SWW, D])
    w_o = _inp(nc, "w_o", [D, D])
    w_up = _inp(nc, "w_up", [D, 2 * DFF])
    convf_in = _inp(nc, "conv_ffnT", [128, 2 * FC * 3])
    w_down = _inp(nc, "w_down", [DFF, D])
    cin = {n: _inp(nc, "c_" + n, list(a.shape)) for n, a in hc.items()}
    out = nc.dram_tensor("out", [OWN, D], F32, kind="ExternalOutput").ap()

    identF = k.sb("identF", [128, 128], F32)
    identB = k.sb("identB", [128, 128], BF16)
    onesF = k.sb("onesF", [128, 128], F32)
    onesB = k.sb("onesB", [128, 128], BF16)
    flag = k.sb("flag", [128, 2], F32)
    modT = k.sb("modT", [128, 6 * KC], F32)
    k.dma(identF[:], cin['identF'])
    k.dma(onesF[:], cin['onesF'])
    k.dma(flag[:], flag_in)
    k.copy(identB[:], identF[:])
    k.copy(onesB[:], onesF[:])
    PS = k.ps

    cact = k.sb("cact", [128, KC], BF16)
    gate_bc = [k.dram("gate_bc%d" % i, [128, D], F32) for i in range(2)]
    with contextlib.ExitStack() as st0:
        k.cur = st0
        k.ring('wp', [128, PIECE], BF16, 8)
        k.ring('wstage', [128, PIECE], F32, 4)
        ctmp = st0.enter_context(nc.sbuf_tensor("ctmp", [128, KC], F32))
        rowsb = [st0.enter_context(nc.sbuf_tensor("rowsb%d" % i, [1, 512], F32)) for i in range(2)]
        bpc = [st0.enter_context(nc.sbuf_tensor("bpc%d" % i, [1, 512], F32)) for i in range(2)]
        gtile = [st0.enter_context(nc.sbuf_tensor("gtile%d" % i, [128, 512], F32)) for i in range(2)]
        k.dma(ctmp[:], cT_in)
        k.act(cact[:], ctmp[:], AF.Silu)
        cnt = [0]

        def epi0(tag, c0, ncols, t0, nt, o):
            i = cnt[0] % 2
            cnt[0] += 1
            k.dma(bpc[i][0:1, 0:ncols], b_ada[0:1, c0:c0 + ncols])
            k.tt(rowsb[i][0:1, 0:ncols], o, bpc[i][0:1, 0:ncols], ALU.add)
            yield
            pc = PS[6][:, 0:8]
            k.mm_multi([(pc[:, 2 * j:2 * j + 2], [(rowsb[i][0:1, j * 128:(j + 1) * 128], onesF[0:1, 0:2])])
                        for j in range(ncols // 128)])
            ch0 = c0 // 128
            k.copy(modT[:, ch0:ch0 + ncols // 128],
                   pc.rearrange("p (j t) -> p j t", t=2)[:, 0:ncols // 128, 0])
            which = c0 // D
            if which in (2, 5):
                pb = PS[7][:, 0:ncols]
                k.mm(pb, [(onesF[0:1, 0:128], rowsb[i][0:1, 0:ncols])])
                cc = c0 - which * D
                gt_ = gtile[i]
                k.copy(gt_[:, 0:ncols], pb, eng='act')
                k.dma(gate_bc[0 if which == 2 else 1][:, cc:cc + ncols], gt_[:, 0:ncols])

        cw = min(512, D)
        gemm(k, 'T', w_ada, D, [(c0, cw, 0) for c0 in range(0, 6 * D, cw)],
             lambda kc, t0, nt: cact[:, kc:kc + 1], [(0, 1)], epi0, [PS[0], PS[1]])
        k.s.barrier()
        k.cur = None
    k.ts(modT[:, KC:2 * KC], modT[:, KC:2 * KC], 1.0, None, op0=ALU.add)
    k.ts(modT[:, 4 * KC:5 * KC], modT[:, 4 * KC:5 * KC], 1.0, None, op0=ALU.add)
    k.dump("dbg_modT", modT[:])

    def norm_stage(tag, tiles, src_fn, scale_ap, shift_ap, dst_fn):
        with contextlib.ExitStack() as stn:
            xt = [stn.enter_context(nc.sbuf_tensor("%s_xt%d" % (tag, i), [128, D], F32)) for i in range(2)]
            junk = stn.enter_context(nc.sbuf_tensor(tag + "_junk", [128, D], BF16))
            ht = [stn.enter_context(nc.sbuf_tensor("%s_ht%d" % (tag, i), [128, KC, 128], BF16)) for i in range(2)]
            sm = [stn.enter_context(nc.sbuf_tensor("%s_sm%d" % (tag, i), [128, 4], F32)) for i in range(2)]
            for n, t in enumerate(tiles):
                x_, h_, s_ = xt[n % 2], ht[n % 2], sm[n % 2]
                k.dma(x_[:], src_fn(t))
                k.act(junk[:], x_[:], AF.Square, accum=s_[:, 0:1])
                k.act(s_[:, 1:2], s_[:, 0:1], AF.Sqrt, bias=EPS, scale=1.0 / D)
                k.recip(s_[:, 2:3], s_[:, 1:2])
                k.ts(x_[:], x_[:], s_[:, 2:3], None, op0=ALU.mult)
                for kc0 in range(0, KC, 4):
                    bank = PS[2 + (kc0 // 4) % 2]
                    k.tr([(bank[:, j * 128:(j + 1) * 128], x_[:, (kc0 + j) * 128:(kc0 + j + 1) * 128])
                          for j in range(4)], identF[:])
                    for j in range(4):
                        kc = kc0 + j
                        if (kc0 // 4) % 2 == 0:
                            k.act(h_[:, kc, :], bank[:, j * 128:(j + 1) * 128], AF.Identity,
                                  bias=shift_ap[:, kc:kc + 1], scale=scale_ap[:, kc:kc + 1])
                        else:
                            k.ts(h_[:, kc, :], bank[:, j * 128:(j + 1) * 128], scale_ap[:, kc:kc + 1],
                                 shift_ap[:, kc:kc + 1], op0=ALU.mult, op1=ALU.add)
                k.dma(dst_fn(t), h_[:])
            k.s.barrier()

    hT1 = [k.dram("hT1_b%d" % b, [128, KC, TBI], BF16) for b in range(NLB)]
    norm_stage("n1", list(range(NT)), lambda t: x_loc[t * 128:(t + 1) * 128, :],
               modT[:, KC:2 * KC], modT[:, 0:KC],
               lambda t: hT1[t // TPBI][:, :, (t % TPBI) * 128:(t % TPBI + 1) * 128])
    k.cfgvals = dict(NT=NT, NLB=NLB, TPB=TPB, TPBI=TPBI, NOB=NOB, LB0=LB0, HT=HT, NFPT=NFPT, NFPTOK=NFPTOK, NMASK=NMASK)
    _stage_inproj(nc, k, cfg, locals())


def _stage_inproj(nc, k, cfg, L):
    D, KC, NTOK, PRE, TB, HA, HS, NG = cfg.D, cfg.KC, cfg.NTOK, cfg.PRE, cfg.TB, cfg.HA, cfg.HS, cfg.NG
    DNW, SWW = cfg.DNW, cfg.SWW
    V = k.cfgvals
    NT, NLB, TPB, NOB, LB0, HT, NFPT, NFPTOK = V['NT'], V['NLB'], V['TPBI'], V['NOB'], V['LB0'], V['HT'], V['NFPT'], V['NFPTOK']
    TB = cfg.TBI
    PS, flag, identB, identF, onesF = k.ps, L['flag'], L['identB'], L['identF'], L['onesF']
    w_in, hT1, cin = L['w_in'], L['hT1'], L['cin']
    NQC = 3 * HA
    S = {}
    S['dnq'] = [k.dram("dnq_b%d" % b, [128, NQC, TB], BF16) for b in range(NLB)]
    S['bg'] = k.dram("bg", [NT, 128, 2 * HA], F32)
    S['zs'] = k.dram("zs", [NFPT, 128, DNW], F32)
    S['swk'] = [k.dram("swk_g%d" % g, [128, HS, NTOK], BF16) for g in range(NG)]
    S['swq'] = [k.dram("swq_g%d" % g, [128, HS, NFPTOK], BF16) for g in range(NG)]
    S['swv'] = [k.dram("swv_g%d" % g, [NT, 128, HS, 128], BF16) for g in range(NG)]
    S['gT'] = k.dram("gT", [2 * KC, 128, NFPTOK], BF16)
    k.S = S
    with contextlib.ExitStack() as sx:
        k.cur = sx
        sbt = lambda name, shape, dt: sx.enter_context(nc.sbuf_tensor(name, list(shape), dt))
        cosT = sbt("ip_cos", [128, NT, 16], F32)
        sinT = sbt("ip_sin", [128, NT, 16], F32)
        with contextlib.ExitStack() as sr:
            sbr = lambda name, shape, dt: sr.enter_context(nc.sbuf_tensor(name, list(shape), dt))
            invf = sbr("ip_invf", [128, 16], F32)
            posi = sbr("ip_posi", [128, NT], I32)
            posf = sbr("ip_posf", [128, NT], F32)
            rtmp = sbr("ip_rtmp", [128, NT, 16], F32)
            rtmp2 = sbr("ip_rtmp2", [128, NT, 16], F32)
            rtmpi = sbr("ip_rtmpi", [128, NT, 16], I32)
            k.dma(invf[:], cin['invf'])
            k.dma(posi[:], L['pos_in'])
            k.copy(posf[:], posi[:])
            k.tt(rtmp[:], posf[:].unsqueeze(2).broadcast_to([128, NT, 16]),
                 invf[:].unsqueeze(1).broadcast_to([128, NT, 16]), ALU.mult)

            def sin_of(dst, shift):
                k.ts(rtmp2[:], rtmp[:], shift, 1.0 / TWO_PI, op0=ALU.add, op1=ALU.mult)
                k.copy(rtmpi[:], rtmp2[:])
                k.copy(rtmp2[:], rtmpi[:])
                k.ts(dst, rtmp[:], shift, None, op0=ALU.add)
                k.stt(dst, rtmp2[:], -CW1, dst, ALU.mult, ALU.add)
                k.stt(dst, rtmp2[:], -CW2, dst, ALU.mult, ALU.add)
                k.ts(rtmp2[:], dst, math.pi, -TWO_PI, op0=ALU.is_gt, op1=ALU.mult)
                k.tt(dst, dst, rtmp2[:], ALU.add)
                k.ts(rtmp2[:], dst, -math.pi, TWO_PI, op0=ALU.is_lt, op1=ALU.mult)
                k.tt(dst, dst, rtmp2[:], ALU.add)
                k.act(dst, dst, AF.Sin)
            sin_of(sinT[:], 0.0)
            sin_of(cosT[:], math.pi / 2)
            k.dump("dbg_cos", cosT[:].rearrange("p a b -> p (a b)"))
            k.dump("dbg_sin", sinT[:].rearrange("p a b -> p (a b)"))
            k.s.barrier()
        k.ring('wp', [128, PIECE], BF16, 8)
        k.ring('wstage', [128, PIECE], F32, 2)
        actb = sbt("ip_act", [128, KC, TB], BF16)
        acth = sbt("ip_acth", [128, KC, 128], BF16)
        convw = sbt("ip_convw", [128, NQC, 4], F32)
        dnhalo = sbt("ip_dnhalo", [128, NQC, 3], F32)
        bgT = sbt("ip_bgT", [128, 2 * KC], F32)
        alog = sbt("ip_alog", [128, HA], F32)
        dtb = sbt("ip_dtb", [128, HA], F32)
        negA = sbt("ip_negA", [128, HA], F32)
        qgain = sbt("ip_qgain", [128, 512], F32)
        kgain = sbt("ip_kgain", [128, 512], F32)
        k.dma(convw[:].rearrange("p a b -> p (a b)"), L['convq_in'])
        k.dma(bgT[:], L['bgate_in'])
        k.dma(alog[:], L['alog_in'])
        k.dma(dtb[:], L['dtb_in'])
        k.dma(qgain[:], L['qgain_in'])
        k.dma(kgain[:], L['kgain_in'])
        k.memset(dnhalo[:], 0.0)
        k.act(negA[:], alog[:], AF.Exp)
        k.ts(negA[:], negA[:], -1.0, None, op0=ALU.mult)

        k.ring('ip_xr', [128, 512 + 4], F32, 2)
        k.ring('ip_acc', [128, 512], F32, 2)
        k.ring('ip_ob', [128, 512], BF16, 3)
        k.ring('ip_t512', [128, 512], F32, 3)
        k.ring('ip_small', [128, 64], F32, 3)
        k.ring('ip_qb', [128, 512], BF16, 7)
        k.ring('ip_qT', [128, 512], BF16, 2)
        ecnt = [0]

        for lb in range(NLB):
            own = lb >= LB0
            first_own = (lb == LB0)
            k.dma(actb[:], hT1[lb])
            if first_own:
                k.dma(acth[:], hT1[lb - 1][:, :, TB - 128:TB])
            t_base = lb * TB
            jobs = []

            def act_fn(kc, t0, nt):
                if t0 < 0:
                    return acth[:, kc, t0 + 128:t0 + 128 + nt]
                return actb[:, kc, t0:t0 + nt]

            def epi_dn(tag, c0, ncols, t0, nt, o):
                c = c0 // 128
                xr = k.nxt('ip_xr')
                acc = k.nxt('ip_acc')
                ob = k.nxt('ip_ob')
                k.copy(xr[:, 0:3], dnhalo[:, c, :], eng='pool')
                if own:
                    k.copy(xr[:, 3:3 + nt], o, eng='act')
                else:
                    k.act(xr[:, 3:3 + nt], o, AF.Copy, scale=flag[:, 0:1])
                k.copy(dnhalo[:, c, :], xr[:, nt:nt + 3], eng='pool')
                k.ts(acc[:, 0:nt], xr[:, 0:nt], convw[:, c, 0:1], None, op0=ALU.mult)
                for j in (1, 2, 3):
                    k.stt(acc[:, 0:nt], xr[:, j:j + nt], convw[:, c, j:j + 1], acc[:, 0:nt], ALU.mult, ALU.add)
                k.act(ob[:, 0:nt], acc[:, 0:nt], AF.Silu)
                k.dma(S['dnq'][lb][:, c, t0:t0 + nt], ob[:, 0:nt])
                return None
                yield

            jobs.append(dict(kind='F', w=w_in, krows=D, cr=[(c0, min(512, 3 * DNW - c0), 0) for c0 in range(0, 3 * DNW, 512)],
                             act_fn=act_fn, toks=[(t_, 512) for t_ in range(0, TB, 512)], epi=epi_dn, banks=[PS[0], PS[1], PS[2], PS[3]]))

            def epi_ba(tag, c0, ncols, t0, nt, o):
                tl = (t_base + t0) // 128
                sm = k.nxt('ip_small')
                k.act(sm[:, 0:HA], o[:, 0:HA], AF.Sigmoid)
                k.tt(sm[:, HA:2 * HA], o[:, HA:2 * HA], dtb[:], ALU.add)
                k.act(sm[:, HA:2 * HA], sm[:, HA:2 * HA], AF.Exp)
                k.act(sm[:, HA:2 * HA], sm[:, HA:2 * HA], AF.Ln, bias=1.0, scale=1.0)
                k.tt(sm[:, HA:2 * HA], sm[:, HA:2 * HA], negA[:], ALU.mult)
                k.dma(S['bg'][tl], sm[:, 0:2 * HA])
                return None
                yield

            ba_job = dict(kind='T', w=w_in, krows=D, cr=[(cfg.c_ba, 2 * HA, 0)], act_fn=act_fn,
                          toks=[(i * 128, 128) for i in range(TPB)], epi=epi_ba, banks=[PS[0], PS[1], PS[2], PS[3]])

            def epi_qk(tag, c0, ncols, t0, nt, o):
                g, which, h0 = tag
                tl = (t_base + t0) // 128
                gain = qgain if which == 0 else kgain
                sq = k.nxt('ip_t512')
                sm = k.nxt('ip_small')
                xn = k.nxt('ip_t512')
                qb = k.nxt('ip_qb')
                k.act(sq[:], o, AF.Square)
                k.reduce(sm[:, 0:4], sq[:].rearrange("p (h d) -> p h d", d=128))
                k.act(sm[:, 4:8], sm[:, 0:4], AF.Sqrt, bias=EPS, scale=1.0 / 128)
                k.recip(sm[:, 8:12], sm[:, 4:8])
                k.tt(xn[:].rearrange("p (h d) -> p h d", d=128), o.rearrange("p (h d) -> p h d", d=128),
                     sm[:, 8:12].unsqueeze(2).broadcast_to([128, 4, 128]), ALU.mult)
                k.tt(xn[:], xn[:], gain[:], ALU.mult, eng='pool')
                x3 = xn[:].rearrange("p (h d) -> p h d", d=128)
                q3 = qb[:].rearrange("p (h d) -> p h d", d=128)
                cs = cosT[:, tl, :].unsqueeze(1).broadcast_to([128, 4, 16])
                sn = sinT[:, tl, :].unsqueeze(1).broadcast_to([128, 4, 16])
                t3 = sq[:].rearrange("p (h d) -> p h d", d=128)
                k.tt(t3[:, :, 0:16], x3[:, :, 0:16], cs, ALU.mult)
                k.tt(t3[:, :, 16:32], x3[:, :, 16:32], sn, ALU.mult)
                k.tt(q3[:, :, 0:16], t3[:, :, 0:16], t3[:, :, 16:32], ALU.subtract)
                k.tt(t3[:, :, 32:48], x3[:, :, 16:32], cs, ALU.mult)
                k.tt(t3[:, :, 48:64], x3[:, :, 0:16], sn, ALU.mult)
                k.tt(q3[:, :, 16:32], t3[:, :, 32:48], t3[:, :, 48:64], ALU.add)
                k.copy(q3[:, :, 32:128], x3[:, :, 32:128], eng='pool')
                yield
                yield
                yield
                yield
                pb = PS[4 + ecnt[0] % 2]
                ecnt[0] += 1
                pbb = pb[:].bitcast(BF16)
                k.tr([(pbb[:, j * 128:(j + 1) * 128], qb[:, j * 128:(j + 1) * 128]) for j in range(4)], identB[:])
                qT = k.nxt('ip_qT')
                k.copy(qT[:], pbb[:, 0:512], eng='act')
                if which == 0:
                    fpt = tl - HT
                    dst = S['swq'][g][:, h0:h0 + 4, fpt * 128:(fpt + 1) * 128]
                else:
                    dst = S['swk'][g][:, h0:h0 + 4, tl * 128:(tl + 1) * 128]
                k.dma(dst, qT[:].rearrange("p (h t) -> p h t", t=128))

            def epi_v(tag, c0, ncols, t0, nt, o):
                g, which, h0 = tag
                tl = (t_base + t0) // 128
                qb = k.nxt('ip_qb')
                if own:
                    k.copy(qb[:], o, eng='act')
                else:
                    k.act(qb[:], o, AF.Copy, scale=flag[:, 0:1])
                k.dma(S['swv'][g][tl][:, h0:h0 + 4, :], qb[:].rearrange("p (h d) -> p h d", d=128))
                return None
                yield

            tiles_all = [(i * 128, 128) for i in range(TPB)]
            tiles_fp = ([(-128, 128)] if first_own else []) + tiles_all
            for g, (w, dl) in enumerate(cfg.GROUPS):
                gbase = cfg.c_sw + g * 3 * SWW
                need_kv = (lb + 1) * TB > cfg.FP0 - w
                if need_kv:
                    jobs.append(dict(kind='T', w=w_in, krows=D, cr=[(gbase + SWW + h0 * 128, 512, (g, 1, h0)) for h0 in range(0, HS, 4)],
                             act_fn=act_fn, toks=tiles_all, epi=epi_qk, banks=[PS[0], PS[1], PS[2], PS[3]]))
                    jobs.append(dict(kind='T', w=w_in, krows=D, cr=[(gbase + 2 * SWW + h0 * 128, 512, (g, 2, h0)) for h0 in range(0, HS, 4)],
                             act_fn=act_fn, toks=tiles_all, epi=epi_v, banks=[PS[0], PS[1], PS[2], PS[3]]))
                if own:
                    jobs.append(dict(kind='T', w=w_in, krows=D, cr=[(gbase + h0 * 128, 512, (g, 0, h0)) for h0 in range(0, HS, 4)],
                             act_fn=act_fn, toks=tiles_fp, epi=epi_qk, banks=[PS[0], PS[1], PS[2], PS[3]]))

            if own:
                def epi_z(tag, c0, ncols, t0, nt, o):
                    fpt = (t_base + t0) // 128 - HT
                    zt = k.nxt('ip_t512')
                    k.act(zt[:, 0:ncols], o, AF.Silu)
                    k.dma(S['zs'][fpt][:, c0 - cfg.c_z:c0 - cfg.c_z + ncols], zt[:, 0:ncols])
                    return None
                    yield
                jobs.append(dict(kind='T', w=w_in, krows=D, cr=[(cfg.c_z + c0, min(512, DNW - c0), 0) for c0 in range(0, DNW, 512)],
                             act_fn=act_fn, toks=tiles_fp, epi=epi_z, banks=[PS[0], PS[1], PS[2], PS[3]]))

                def epi_g(tag, c0, ncols, t0, nt, o):
                    ch = (c0 - cfg.c_gate) // 128
                    ob = k.nxt('ip_ob')
                    k.act(ob[0:ncols, 0:nt], o, AF.Sigmoid, bias=bgT[0:ncols, ch:ch + 1], scale=1.0)
                    f0 = (t_base + t0) - cfg.FP0
                    k.dma(S['gT'][ch][:, f0:f0 + nt], ob[:, 0:nt])
                    return None
                    yield
                jobs.append(dict(kind='F', w=w_in, krows=D, cr=[(cfg.c_gate + c0, 512, 0) for c0 in range(0, 2 * D, 512)],
                             act_fn=act_fn, toks=([(-128, 128)] if first_own else []) + [(t_, 512) for t_ in range(0, TB, 512)], epi=epi_g, banks=[PS[0], PS[1], PS[2], PS[3]]))
            jobs.append(ba_job)
            run_jobs(k, jobs)
        k.s.barrier()
        k.cur = None
    _stage_dn(nc, k, cfg, L)


def _stage_dn(nc, k, cfg, L):
    HA, TB = cfg.HA, cfg.TB
    V = k.cfgvals
    NT, TPB, HT, NFPT, NFPTOK = V['NT'], V['TPBI'], V['HT'], V['NFPT'], V['NFPTOK']
    S, cin = k.S, L['cin']
    identB, identF, onesF = L['identB'], L['identF'], L['onesF']
    NQC = 3 * HA
    DNW = cfg.DNW
    S['oaT'] = k.dram("oaT", [128, HA, NFPTOK], BF16)
    import os as _os
    GH = min(int(_os.environ.get('DN_GH', '8')), HA)
    with contextlib.ExitStack() as sx:
        k.cur = sx
        sbt = lambda name, shape, dt: sx.enter_context(nc.sbuf_tensor(name, list(shape), dt))
        nm_incl = sbt("dn_nmincl", [128, 128], F32)
        m_sneg = sbt("dn_msneg", [128, 128], F32)
        tri = sbt("dn_tri", [128, 128], F32)
        ogain = sbt("dn_ogain", [128, 128], F32)
        k.dma(nm_incl[:], cin['nm_incl'])
        k.dma(m_sneg[:], cin['m_sneg'])
        k.dma(tri[:], cin['tri_incl'])
        k.dma(ogain[:], L['ogain_in'])
        binv = sbt("dn_binv", [128, 7 * 128], F32)
        k.dma(binv[:], cin['binv'])
        Sf = [sbt("dn_Sf%d" % h, [128, 128], F32) for h in range(HA)]
        Sb = [sbt("dn_Sb%d" % h, [128, 128], BF16) for h in range(HA)]
        for h in range(HA):
            k.memset(Sf[h][:], 0.0, eng='pool')
            k.memset(Sb[h][:], 0.0, eng='pool')
        k.ring('dn_q', [128, NQC, 128], BF16, 2)
        k.ring('dn_bg', [128, 2 * HA], F32, 2)
        k.ring('dn_sc', [128, 6, HA], F32, 2)
        k.ring('dn_z', [128, DNW], F32, 2)
        k.ring('dn_oa', [128, HA, 128], BF16, 2)
        k.ring('dn_oaT', [128, HA, 128], BF16, 2)
        R = GH + 1
        k.ring('dn_tok', [128, 7, 128], BF16, R)
        k.ring('dn_ft', [128, 4, 128], BF16, R)
        k.ring('dn_sm', [128, 16], F32, R)
        k.ring('dn_gbc', [128, 128], F32, R)
        k.ring('dn_arg', [128, 128], F32, GH)
        k.ring('dn_gam', [128, 128], F32, R)
        k.ring('dn_gsn', [128, 128], F32, GH)
        k.ring('dn_xn', [128, 2, 128], F32, 2 * GH + 2)
        k.ring('dn_xoff', [128, 2, 128], F32, 3 * GH)
        k.ring('dn_pf', [128, 128], F32, 4 * GH)
        k.ring('dn_pb', [128, 128], BF16, R)
        k.ring('dn_u', [128, 128], F32, R)
        k.ring('dn_wT', [128, 128], BF16, R)
        k.ring('dn_aqk', [128, 128], BF16, R)
        k.ring('dn_vn', [128, 128], BF16, R)
        k.ring('dn_ot', [128, 128], F32, R)
        k.ring('dn_junk', [128, 128], BF16, 4)
        PS = k.ps
        qi = [0]

        def psq(n=1):
            q = qi[0] % 8
            qi[0] += 1
            return PS[q][:, 0:n * 128]

        for c in range(NT):
            fp = c >= HT
            fpt = c - HT
            lb, off = c // TPB, (c % TPB) * 128
            Q = k.nxt('dn_q')
            bgt = k.nxt('dn_bg')
            sc = k.nxt('dn_sc')
            k.dma(Q[:], S['dnq'][lb][:, :, off:off + 128])
            k.dma(bgt[:], S['bg'][c])
            if fp:
                zt = k.nxt('dn_z')
                k.dma(zt[:], S['zs'][fpt])
                oa = k.nxt('dn_oa')
            pg = psq(1)
            k.mm(pg[:, 0:HA], [(tri[:], bgt[:, HA:2 * HA])])
            k.mm(pg[:, HA:2 * HA], [(onesF[:], bgt[:, HA:2 * HA])])
            k.copy(sc[:, 0, :], bgt[:, 0:HA], eng='pool')
            k.copy(sc[:, 4, :], pg[:, 0:HA])
            k.act(sc[:, 3, :], pg[:, 0:HA], AF.Exp)
            k.act(sc[:, 5, :], pg[:, HA:2 * HA], AF.Exp)
            k.tt(sc[:, 2, :], pg[:, HA:2 * HA], sc[:, 4, :], ALU.subtract)
            k.act(sc[:, 2, :], sc[:, 2, :], AF.Exp)
            k.tt(sc[:, 1, :], sc[:, 0, :], sc[:, 3, :], ALU.mult, eng='pool')

            for h0 in range(0, HA, GH):
                hs = list(range(h0, min(HA, h0 + GH)))
                T = {}
                for h in hs:
                    d = T[h] = {}
                    p1 = psq(2).bitcast(BF16)
                    k.tr([(p1[:, 0:128], Q[:, HA + h, :]), (p1[:, 128:256], Q[:, h, :]),
                          (p1[:, 256:384], Q[:, 2 * HA + h, :])], identB[:])
                    d['p1'] = p1
                for h in hs:
                    d = T[h]
                    p1 = d['p1']
                    sm = d['sm'] = k.nxt('dn_sm')
                    j1, j2 = k.nxt('dn_junk'), k.nxt('dn_junk')
                    k.act(j1[:], p1[:, 0:128], AF.Square, accum=sm[:, 0:1])
                    k.act(j2[:], p1[:, 128:256], AF.Square, accum=sm[:, 1:2])
                    k.act(sm[:, 2:4], sm[:, 0:2], AF.Sqrt, bias=EPS, scale=1.0)
                    k.recip(sm[:, 4:6], sm[:, 2:4])
                    k.ts(sm[:, 6:9], sc[:, 0:3, h], sm[:, 4:5], None, op0=ALU.mult)
                    k.ts(sm[:, 9:10], sm[:, 5:6], 128.0 ** -0.5, None, op0=ALU.mult)
                    k.tt(sm[:, 10:11], sm[:, 9:10], sc[:, 3, h:h + 1], ALU.mult)
                for h in hs:
                    d = T[h]
                    p1, sm = d['p1'], d['sm']
                    tk = d['tk'] = k.nxt('dn_tok')
                    k.ts(tk[:, 1, :], p1[:, 0:128], sm[:, 6:7], None, op0=ALU.mult)
                    k.ts(tk[:, 3, :], p1[:, 0:128], sm[:, 8:9], None, op0=ALU.mult)
                    k.ts(tk[:, 6, :], p1[:, 256:384], sc[:, 0, h:h + 1], None, op0=ALU.mult)
                    if fp:
                        k.ts(tk[:, 5, :], p1[:, 128:256], sm[:, 10:11], None, op0=ALU.mult)
                    k.act(tk[:, 0, :], p1[:, 0:128], AF.Copy, scale=sm[:, 4:5])
                    k.act(tk[:, 2, :], p1[:, 0:128], AF.Copy, scale=sm[:, 7:8])
                    if fp:
                        k.act(tk[:, 4, :], p1[:, 128:256], AF.Copy, scale=sm[:, 9:10])
                    gbc = d['gbc'] = k.nxt('dn_gbc')
                    k.ts(gbc[:], onesF[:], bgt[:, HA + h:HA + h + 1], None, op0=ALU.mult, eng='pool')
                for h in hs:
                    d = T[h]
                    tk = d['tk']
                    p2 = psq(2).bitcast(BF16)
                    nb = 4 if fp else 2
                    srcs = [tk[:, 0, :], tk[:, 1, :], tk[:, 4, :], tk[:, 5, :]][:nb]
                    k.tr([(p2[:, j * 128:(j + 1) * 128], srcs[j]) for j in range(nb)], identB[:])
                    ft = d['ft'] = k.nxt('dn_ft')
                    k.copy(ft[:, 0:nb, :], p2[:, 0:nb * 128].rearrange("p (a b) -> p a b", b=128), eng='act')
                    pG = psq(1)
                    k.mm(pG, [(d['gbc'][:], tri[:])])
                    arg = k.nxt('dn_arg')
                    k.stt(arg[:], pG, sc[:, 4, h:h + 1], nm_incl[:], ALU.subtract, ALU.add)
                    gam = d['gam'] = k.nxt('dn_gam')
                    k.act(gam[:], arg[:], AF.Exp)
                    gsn = d['gsn'] = k.nxt('dn_gsn')
                    k.tt(gsn[:], gam[:], m_sneg[:], ALU.mult, eng='pool')
                for h in hs:
                    d = T[h]
                    ft = d['ft']
                    pA = psq(1)
                    k.mm(pA, [(ft[:, 0, :], ft[:, 1, :])])
                    xn = d['xn'] = k.nxt('dn_xn')
                    k.tt(xn[:, 0, :], pA, d['gsn'][:], ALU.mult)
                for h in hs:
                    d = T[h]
                    xn = d['xn']
                    pN = psq(1)
                    k.tr([(pN, xn[:, 0, :])], identF[:])
                    k.copy(xn[:, 1, :], pN, eng='act')
                    xd = d['xd'] = k.nxt('dn_xn')
                    k.tt(xd[:], xn[:], binv[:, 0:128].unsqueeze(1).broadcast_to([128, 2, 128]), ALU.mult, eng='pool')
                    d['off'] = []
                    for li in range(3):
                        xo = k.nxt('dn_xoff')
                        k.tt(xo[:, 0, :], xn[:, 0, :], binv[:, (1 + li) * 128:(2 + li) * 128], ALU.mult, eng='pool')
                        k.tt(xo[:, 1, :], xn[:, 1, :], binv[:, (4 + li) * 128:(5 + li) * 128], ALU.mult, eng='pool')
                        d['off'].append(xo)
                    pf = d['pf'] = k.nxt('dn_pf')
                    k.tt(pf[:], xd[:, 0, :], identF[:], ALU.add)
                for lvl in range(1, 4):
                    for h in hs:
                        d = T[h]
                        xo = d['xd']
                        pX = psq(2)
                        xn2 = k.nxt('dn_xn')
                        if lvl < 3:
                            k.mm_multi([(pX[:, 0:128], [(xo[:, 1, :], xo[:, 0, :])]),
                                        (pX[:, 128:256], [(xo[:, 0, :], xo[:, 1, :])])])
                            k.copy(xn2[:].rearrange("p a b -> p (a b)"), pX, eng=('act' if h % 2 else 'dve'))
                        else:
                            k.mm(pX[:, 128:256], [(xo[:, 0, :], xo[:, 1, :])])
                            k.copy(xn2[:, 1, :], pX[:, 128:256], eng=('act' if h % 2 else 'dve'))
                        d['xd'] = xn2
                    for h in hs:
                        d = T[h]
                        pP = psq(1)
                        k.mm(pP, [(d['xd'][:, 1, :], d['pf'][:])])
                        pf2 = k.nxt('dn_pf')
                        k.tt(pf2[:], pP, d['pf'][:], ALU.add)
                        d['pf'] = pf2
                for li in range(3):
                    for h in hs:
                        d = T[h]
                        pT = psq(1)
                        k.tr([(pT, d['pf'][:])], identF[:])
                        td = d['td'] = k.nxt('dn_pf')
                        k.copy(td[:], pT, eng='act')
                        pM = psq(1)
                        k.mm(pM, [(d['off'][li][:, 1, :], d['pf'][:])])
                        m1 = d['m1'] = k.nxt('dn_pf')
                        k.copy(m1[:], pM, eng=('dve' if h % 2 else 'act'))
                    for h in hs:
                        d = T[h]
                        pM2 = psq(1)
                        k.mm(pM2, [(d['td'][:], d['m1'][:])])
                        pf2 = k.nxt('dn_pf')
                        k.tt(pf2[:], pM2, d['pf'][:], ALU.add)
                        d['pf'] = pf2
                for h in hs:
                    d = T[h]
                    pb = d['pb'] = k.nxt('dn_pb')
                    k.copy(pb[:], d['pf'][:], eng='pool')
                for h in hs:
                    d = T[h]
                    tk, ft, pb = d['tk'], d['ft'], d['pb']
                    pU = psq(2)
                    k.mm_multi([(pU[:, 0:128], [(pb[:], tk[:, 6, :])]),
                                (pU[:, 128:256], [(tk[:, 2, :], pb[:])])])
                    u = d['u'] = k.nxt('dn_u')
                    wT = d['wT'] = k.nxt('dn_wT')
                    k.copy(u[:], pU[:, 0:128], eng=('act' if h % 2 else 'dve'))
                    k.copy(wT[:], pU[:, 128:256], eng=('act' if h % 2 else 'dve'))
                    if fp:
                        pQ = psq(1)
                        k.mm(pQ, [(ft[:, 0, :], ft[:, 2, :])])
                        aqk = d['aqk'] = k.nxt('dn_aqk')
                        k.tt(aqk[:], pQ, d['gam'][:], ALU.mult)
                for h in hs:
                    d = T[h]
                    pV = psq(1)
                    k.mm(pV, [(d['wT'][:], Sb[h][:])])
                    vn = d['vn'] = k.nxt('dn_vn')
                    k.tt(vn[:], d['u'][:], pV, ALU.subtract)
                for s0 in range(0, len(hs), 4):
                    sub = hs[s0:s0 + 4]
                    for h in sub:
                        d = T[h]
                        if fp:
                            pO = d['pO'] = psq(1)
                            k.mm(pO, [(d['ft'][:, 3, :], Sb[h][:]), (d['aqk'][:], d['vn'][:])])
                        pS = psq(1)
                        k.mm(pS, [(d['tk'][:, 3, :], d['vn'][:])])
                        k.stt(Sf[h][:], Sf[h][:], sc[:, 5, h:h + 1], pS, ALU.mult, ALU.add)
                        k.copy(Sb[h][:], Sf[h][:], eng='act')
                    if fp:
                        for h in sub:
                            d = T[h]
                            sm, pO = d['sm'], d['pO']
                            jk = k.nxt('dn_junk')
                            k.act(jk[:], pO, AF.Square, accum=sm[:, 11:12])
                            k.act(sm[:, 12:13], sm[:, 11:12], AF.Sqrt, bias=EPS, scale=1.0 / 128)
                            k.recip(sm[:, 13:14], sm[:, 12:13])
                            ot = k.nxt('dn_ot')
                            k.stt(ot[:], pO, sm[:, 13:14], ogain[:], ALU.mult, ALU.mult)
                            k.tt(oa[:, h, :], ot[:], zt[:, h * 128:(h + 1) * 128], ALU.mult, eng='pool')
            if fp:
                oaTs = k.nxt('dn_oaT')
                for h0 in range(0, HA, 4):
                    pT = psq(2).bitcast(BF16)
                    nb = min(4, HA - h0)
                    k.tr([(pT[:, j * 128:(j + 1) * 128], oa[:, h0 + j, :]) for j in range(nb)], identB[:])
                    k.copy(oaTs[:, h0:h0 + nb, :], pT[:, 0:nb * 128].rearrange("p (a b) -> p a b", b=128), eng='act')
                k.dma(S['oaT'][:, :, fpt * 128:(fpt + 1) * 128], oaTs[:])
        k.s.barrier()
        k.cur = None
    _stage_attn(nc, k, cfg, L)


def _stage_attn(nc, k, cfg, L):
    HS, NG, NTOK, PRE = cfg.HS, cfg.NG, cfg.NTOK, cfg.PRE
    V = k.cfgvals
    NT, HT, NFPT, NFPTOK, NMASK = V['NT'], V['HT'], V['NFPT'], V['NFPTOK'], V['NMASK']
    S, cin, flag, identB = k.S, L['cin'], L['flag'], L['identB']
    S['obT'] = k.dram("obT", [128, HS, NFPTOK], BF16)
    PS = k.ps
    with contextlib.ExitStack() as sx:
        k.cur = sx
        sbt = lambda name, shape, dt: sx.enter_context(nc.sbuf_tensor(name, list(shape), dt))
        amask = sbt("at_mask", [128, NMASK * 128], F32)
        k.dma(amask[:], cin['amask'])
        KT = [sbt("at_K%d" % i, [128, NG, NTOK], BF16) for i in range(2)]
        QT = [sbt("at_Q%d" % i, [128, NG, NFPTOK], BF16) for i in range(2)]
        VT = [sbt("at_V%d" % i, [128, NG, NT, 132], BF16) for i in range(2)]
        for i in range(2):
            for g_ in range(NG):
                k.copy(VT[i][:, g_, :, 128:130], L['onesB'][:, 0:2 * NT].rearrange("p (a b) -> p a b", b=2))
        k.ring('at_e', [128, 512], F32, 4)
        k.ring('at_pm', [128, 512], BF16, 4)
        k.ring('at_sm', [128, 4], F32, 2)
        k.ring('at_ob', [128, 128], BF16, 2)
        k.ring('at_obT', [128, NFPTOK], BF16, 2)
        sbank = [0]
        scale = 128.0 ** -0.5
        for h in range(HS):
            Kt, Qt, Vt = KT[h % 2], QT[h % 2], VT[h % 2]
            for g in range(NG):
                k.dma(Kt[:, g, :], S['swk'][g][:, h, :])
                k.dma(Qt[:, g, :], S['swq'][g][:, h, :])
                k.dma(Vt[:, g, :, 0:128], S['swv'][g][:, :, h, :].rearrange("t p d -> p t d"))
            obT = k.nxt('at_obT')
            for f in range(NFPT):
                qt = HT + f
                batches = []
                for g, (w, dl) in enumerate(cfg.GROUPS):
                    js = [j for j in range(dl + 1) if qt - j >= 0]
                    for cls in (0, 1):
                        sel = [j for j in js if (((qt - j) * 128 < PRE) == bool(cls))]
                        for i0 in range(0, len(sel), 4):
                            batches.append((g, cls, sel[i0:i0 + 4]))
                npv = sum(len(b_[2]) for b_ in batches)
                pO = PS[4 + f % 2][:, 0:130]
                pvc = [0]

                def stage_a(g, cls, js):
                    n = len(js)
                    pS = PS[sbank[0] % 4][:, 0:n * 128]
                    sbank[0] += 1
                    k.mm_multi([(pS[:, i * 128:(i + 1) * 128],
                                 [(Kt[:, g, (qt - j) * 128:(qt - j + 1) * 128], Qt[:, g, f * 128:(f + 1) * 128])])
                                for i, j in enumerate(js)])
                    e = k.nxt('at_e')
                    if cls:
                        k.act(e[:, 0:n * 128], pS, AF.Exp, bias=flag[:, 1:2], scale=scale)
                    else:
                        k.act(e[:, 0:n * 128], pS, AF.Exp, scale=scale)
                    pm = k.nxt('at_pm')
                    mi = cfg.mask_idx[(g, js[0])]
                    assert all(cfg.mask_idx[(g, j)] == mi + i for i, j in enumerate(js))
                    k.tt(pm[:, 0:n * 128], e[:, 0:n * 128], amask[:, mi * 128:(mi + n) * 128], ALU.mult,
                         eng=('dve' if sbank[0] % 2 else 'pool'))
                    return pm, [Vt[:, g, qt - j, 0:130] for j in js]

                def stage_b(pm, vvs):
                    fns, reads = [], []
                    for i, vv in enumerate(vvs):
                        first, last = (pvc[0] == 0), (pvc[0] == npv - 1)
                        pvc[0] += 1
                        l = pm[:, i * 128:(i + 1) * 128]
                        fns.append(lambda e_, l=l, vv=vv, first=first, last=last, pO=pO:
                                   e_.matmul(pO, l, vv, start=first, stop=last))
                        reads += [l, vv]
                    k.s.op('pe', fns, k._aps(*reads), k._aps(pO))

                prev = []
                for (g, cls, js) in batches:
                    prev.append(stage_a(g, cls, js))
                    if len(prev) > 2:
                        pm0, v0 = prev.pop(0)
                        stage_b(pm0, v0)
                for pm0, v0 in prev:
                    stage_b(pm0, v0)
                sm = k.nxt('at_sm')
                if h == 0 and f == 1 and 'dbg_pO' in k.debug:
                    dbt = sbt("at_dbg", [128, 132], F32)
                    k.copy(dbt[:, 0:130], pO)
                    k.dump('dbg_pO', dbt[:])
                    dbv = sbt("at_dbgv", [128, 132], F32)
                    k.copy(dbv[:], Vt[:, 0, qt, :])
                    k.dump('dbg_V', dbv[:])
                k.ts(sm[:, 0:1], pO[:, 128:129], 1e-30, None, op0=ALU.add)
                k.recip(sm[:, 1:2], sm[:, 0:1])
                ob = k.nxt('at_ob')
                k.ts(ob[:], pO[:, 0:128], sm[:, 1:2], None, op0=ALU.mult)
                pT = PS[6 + f % 2][:].bitcast(BF16)
                k.tr([(pT[:, 0:128], ob[:])], identB[:])
                k.copy(obT[:, f * 128:(f + 1) * 128], pT[:, 0:128], eng='act')
            k.dma(S['obT'][:, h, :], obT[:])
        k.s.barrier()
        k.cur = None
    _stage_out(nc, k, cfg, L)


def _stage_out(nc, k, cfg, L):
    D, KC, TB, HA, HS, DFF, FC, PRE = cfg.D, cfg.KC, cfg.TB, cfg.HA, cfg.HS, cfg.DFF, cfg.FC, cfg.PRE
    V = k.cfgvals
    NT, TPB, NOB, HT, NFPT, NFPTOK = V['NT'], V['TPB'], V['NOB'], V['HT'], V['NFPT'], V['NFPTOK']
    S, flag, modT, gate_bc, x_loc = k.S, L['flag'], L['modT'], L['gate_bc'], L['x_loc']
    PS = k.ps
    S['x1'] = k.dram("x1", [NFPT, 128, D], F32)
    TBX = TB + 128
    blocks = [(0, TBX)] + [(128 + ob * TB, TB) for ob in range(1, NOB)]
    with contextlib.ExitStack() as sx:
        k.cur = sx
        k.ring('wp', [128, PIECE], BF16, 8)
        k.ring('wstage', [128, PIECE], F32, 2)
        sbt = lambda name, shape, dt: sx.enter_context(nc.sbuf_tensor(name, list(shape), dt))
        actA = sbt("o_actA", [128, HA, TBX], BF16)
        actB = sbt("o_actB", [128, HS, TBX], BF16)
        mT = sbt("o_mT", [128, KC, TBX], BF16)
        gmix = sbt("o_gmix", [128, D], F32)
        k.dma(gmix[:], gate_bc[0])
        k.ring('o_g', [128, TB], BF16, 3)
        k.ring('o_t', [128, TB], F32, 2)
        k.ring('o_x', [128, 512], F32, 2)
        k.ring('o_x1', [128, 512], F32, 2)
        for (f0, n) in blocks:
            k.dma(actA[:, :, 0:n], S['oaT'][:, :, f0:f0 + n])
            k.dma(actB[:, :, 0:n], S['obT'][:, :, f0:f0 + n])

            def epi_a(tag, c0, ncols, t0, nt, o):
                ch = c0 // 128
                g = k.nxt('o_g')
                k.dma(g[:, 0:nt], S['gT'][ch][:, f0 + t0:f0 + t0 + nt])
                k.tt(mT[:, ch, t0:t0 + nt], o, g[:, 0:nt], ALU.mult)
                return None
                yield

            def epi_b(tag, c0, ncols, t0, nt, o):
                ch = c0 // 128
                g = k.nxt('o_g')
                t = k.nxt('o_t')
                k.dma(g[:, 0:nt], S['gT'][KC + ch][:, f0 + t0:f0 + t0 + nt])
                k.tt(t[:, 0:nt], o, g[:, 0:nt], ALU.mult)
                k.tt(mT[:, ch, t0:t0 + nt], mT[:, ch, t0:t0 + nt], t[:, 0:nt], ALU.add, eng='pool')
                return None
                yield

            cr = [(c0, min(512, D - c0), 0) for c0 in range(0, D, 512)]
            ftoks = [(0, 128), (128, TB)] if n == TBX else [(0, n)]
            jobs = [dict(kind='F', w=L['w_a'], krows=cfg.DNW, cr=cr, act_fn=lambda kc, t0, nt: actA[:, kc, t0:t0 + nt],
                         toks=ftoks, epi=epi_a, banks=[PS[0], PS[1], PS[4], PS[5]]),
                    dict(kind='F', w=L['w_b'], krows=cfg.SWW, cr=cr, act_fn=lambda kc, t0, nt: actB[:, kc, t0:t0 + nt],
                         toks=ftoks, epi=epi_b, banks=[PS[0], PS[1], PS[4], PS[5]])]

            def epi_o(tag, c0, ncols, t0, nt, o):
                fpt = (f0 + t0) // 128
                lt = HT + fpt
                xt = k.nxt('o_x')
                x1t = k.nxt('o_x1')
                k.dma(xt[:, 0:ncols], x_loc[lt * 128:(lt + 1) * 128, c0:c0 + ncols])
                k.tt(x1t[:, 0:ncols], o, gmix[:, c0:c0 + ncols], ALU.mult)
                k.tt(x1t[:, 0:ncols], x1t[:, 0:ncols], xt[:, 0:ncols], ALU.add, eng='pool')
                k.dma(S['x1'][fpt][:, c0:c0 + ncols], x1t[:, 0:ncols])
                return None
                yield
            jobs.append(dict(kind='T', w=L['w_o'], krows=D, cr=cr, act_fn=lambda kc, t0, nt: mT[:, kc, t0:t0 + nt],
                             toks=[(i * 128, 128) for i in range(n // 128)], epi=epi_o, banks=[PS[2], PS[3], PS[6], PS[7]]))
            run_jobs(k, jobs)
        k.s.barrier()
        k.cur = None

    S['h2h'] = k.dram("h2T_halo", [128, KC, 128], BF16)
    TBU = cfg.TBI
    TPBU = TBU // 128
    NUB = cfg.OWN // TBU
    S['h2'] = [k.dram("h2T_b%d" % b, [128, KC, TBU], BF16) for b in range(NUB)]

    def dst2(t):
        if t == 0:
            return S['h2h']
        return S['h2'][(t - 1) // TPBU][:, :, ((t - 1) % TPBU) * 128:((t - 1) % TPBU + 1) * 128]
    L['norm_stage']("n2", list(range(NFPT)), lambda t: S['x1'][t], modT[:, 4 * KC:5 * KC], modT[:, 3 * KC:4 * KC], dst2)

    S['ffa'] = [k.dram("ffa_b%d" % b, [128, FC, TB], BF16) for b in range(NOB)]
    with contextlib.ExitStack() as sx:
        k.cur = sx
        k.ring('wp', [128, PIECE], BF16, 8)
        k.ring('wstage', [128, PIECE], F32, 2)
        sbt = lambda name, shape, dt: sx.enter_context(nc.sbuf_tensor(name, list(shape), dt))
        actb = sbt("f_act", [128, KC, TBU], BF16)
        acth = sbt("f_acth", [128, KC, 128], BF16)
        convw = sbt("f_convw", [128, 2 * FC, 3], F32)
        ffhalo = sbt("f_halo", [128, 2 * FC, 2], F32)
        gbuf = sbt("f_gbuf", [128, 4, TBU], F32)
        k.dma(convw[:].rearrange("p a b -> p (a b)"), L['convf_in'])
        k.dma(acth[:], S['h2h'])
        k.ring('f_xr', [128, 512 + 2], F32, 2)
        k.ring('f_acc', [128, 512], F32, 2)
        k.ring('f_ob', [128, 512], BF16, 3)
        for ob in range(NUB):
            k.dma(actb[:], S['h2'][ob])

            def act_fn(kc, t0, nt):
                if t0 < 0:
                    return acth[:, kc, 128 + t0:128 + t0 + nt]
                return actb[:, kc, t0:t0 + nt]

            def epi_up(tag, c0, ncols, t0, nt, o):
                c = c0 // 128
                if t0 < 0:
                    k.act(ffhalo[0:ncols, c, :], o, AF.Copy, scale=flag[0:ncols, 0:1])
                    return None
                xr = k.nxt('f_xr')
                acc = k.nxt('f_acc')
                k.copy(xr[:, 0:2], ffhalo[:, c, :], eng='pool')
                k.copy(xr[:, 2:2 + nt], o, eng='act')
                k.copy(ffhalo[:, c, :], xr[:, nt:nt + 2], eng='pool')
                k.ts(acc[:, 0:nt], xr[:, 0:nt], convw[:, c, 0:1], None, op0=ALU.mult)
                for j in (1, 2):
                    k.stt(acc[:, 0:nt], xr[:, j:j + nt], convw[:, c, j:j + 1], acc[:, 0:nt], ALU.mult, ALU.add)
                if tag[0] == 'g':
                    k.act(gbuf[:, (c - tag[1]), t0:t0 + nt], acc[:, 0:nt], AF.Silu)
                else:
                    obt = k.nxt('f_ob')
                    cc = c - FC
                    k.tt(obt[:, 0:nt], acc[:, 0:nt], gbuf[:, cc - tag[1], t0:t0 + nt], ALU.mult, eng='pool')
                    tok_ = ob * TBU + t0
                    k.dma(S['ffa'][tok_ // TB][:, cc, tok_ % TB:tok_ % TB + nt], obt[:, 0:nt])
                return None
                yield
            cr = []
            for c0 in range(0, DFF, 512):
                w_ = min(512, DFF - c0)
                cr.append((c0, w_, ('g', c0 // 128)))
                cr.append((DFF + c0, w_, ('v', c0 // 128)))
            gemm(k, 'F', L['w_up'], D, cr, act_fn, ([(-2, 2)] if ob == 0 else []) + [(t_, 512) for t_ in range(0, TBU, 512)], epi_up, [PS[0], PS[1], PS[2], PS[3]])
        k.s.barrier()
        k.cur = None

    with contextlib.ExitStack() as sx:
        k.cur = sx
        k.ring('wp', [128, PIECE], BF16, 5)
        k.ring('wstage', [128, PIECE], F32, 2)
        sbt = lambda name, shape, dt: sx.enter_context(nc.sbuf_tensor(name, list(shape), dt))
        actf = sbt("d_act", [128, FC, TB], BF16)
        gffn = sbt("d_gffn", [128, D], F32)
        k.dma(gffn[:], gate_bc[1])
        k.ring('d_x', [128, 512], F32, 2)
        k.ring('d_y', [128, 512], F32, 2)
        for ob in range(NOB):
            k.dma(actf[:], S['ffa'][ob])

            def epi_dn(tag, c0, ncols, t0, nt, o):
                own_t = ob * TPB + t0 // 128
                fpt = own_t + 1
                xt = k.nxt('d_x')
                yt = k.nxt('d_y')
                k.dma(xt[:, 0:ncols], S['x1'][fpt][:, c0:c0 + ncols])
                k.tt(yt[:, 0:ncols], o, gffn[:, c0:c0 + ncols], ALU.mult)
                k.tt(yt[:, 0:ncols], yt[:, 0:ncols], xt[:, 0:ncols], ALU.add, eng='pool')
                k.dma(L['out'][own_t * 128:(own_t + 1) * 128, c0:c0 + ncols], yt[:, 0:ncols])
                return None
                yield
            gemm(k, 'T', L['w_down'], DFF, [(c0, min(512, D - c0), 0) for c0 in range(0, D, 512)],
                 lambda kc, t0, nt: actf[:, kc, t0:t0 + nt], [(i * 128, 128) for i in range(TPB)], epi_dn,
                 [PS[0], PS[1], PS[2], PS[3]])
        k.s.barrier()
        k.cur = None
    _finish(nc, k, cfg, L)


def _finish(nc, k, cfg, L):
    k.s.final_wait('sp')
    with nc.Block() as block:
        k.s.emit(block)


def make_in_maps(cfg, inputs):
    D, KC, NTOK, PRE, HA, HS, FC = cfg.D, cfg.KC, cfg.NTOK, cfg.PRE, cfg.HA, cfg.HS, cfg.FC
    f = lambda a: np.ascontiguousarray(np.asarray(a, dtype=np.float32))
    x = np.asarray(inputs['x'], dtype=np.float32)
    c = np.asarray(inputs['c'], dtype=np.float32)
    pos = np.asarray(inputs['positions']).astype(np.int32)
    hc = host_consts(cfg)
    shared = {
        'w_ada': f(inputs['w_ada'][0]), 'b_ada': f(inputs['b_ada'][0]).reshape(1, -1),
        'w_in': f(inputs['w_in'][0]),
        'b_gateT': f(np.asarray(inputs['b_gate'][0]).reshape(2 * KC, 128).T),
        'conv_qkvT': f(np.asarray(inputs['conv_qkv'][0]).reshape(4, 3 * HA, 128).transpose(2, 1, 0).reshape(128, -1)),
        'a_log_bc': f(np.broadcast_to(np.asarray(inputs['a_log'][0])[None, :], (128, HA))),
        'dt_bias_bc': f(np.broadcast_to(np.asarray(inputs['dt_bias'][0])[None, :], (128, HA))),
        'o_gain_bc': f(np.broadcast_to(np.asarray(inputs['o_norm_gain'][0])[None, :], (128, 128))),
        'q_gain_bc': f(np.broadcast_to(np.tile(np.asarray(inputs['q_norm_gain'][0]), 4)[None, :], (128, 512))),
        'k_gain_bc': f(np.broadcast_to(np.tile(np.asarray(inputs['k_norm_gain'][0]), 4)[None, :], (128, 512))),
        'w_a_proj': f(inputs['w_a_proj'][0]), 'w_b_proj': f(inputs['w_b_proj'][0]), 'w_o': f(inputs['w_o'][0]),
        'w_up': f(inputs['w_up'][0]),
        'conv_ffnT': f(np.asarray(inputs['conv_ffn'][0]).reshape(3, 2 * FC, 128).transpose(2, 1, 0).reshape(128, -1)),
        'w_down': f(inputs['w_down'][0]),
    }
    for n, a in hc.items():
        shared['c_' + n] = f(a)
    maps = []
    for core in range(2 * cfg.B):
        b, half = core // 2, core % 2
        m = dict(shared)
        if half == 1:
            xl = x[b, 0:NTOK]
            pl = pos[b, 0:NTOK]
        else:
            xl = np.concatenate([np.zeros((PRE, D), np.float32), x[b, 0:NTOK - PRE]], 0)
            pl = np.concatenate([np.zeros((PRE,), np.int32), pos[b, 0:NTOK - PRE]], 0)
        m['x_loc'] = np.ascontiguousarray(xl)
        m['cT'] = np.ascontiguousarray(c[b].reshape(KC, 128).T)
        m['posT'] = np.ascontiguousarray(pl.reshape(NTOK // 128, 128).T)
        fl = np.zeros((128, 2), np.float32)
        fl[:, 0] = 1.0 if half == 1 else 0.0
        fl[:, 1] = 0.0 if half == 1 else NEG
        m['flagP'] = fl
        maps.append(m)
    return maps


def run(cfg, inputs, debug=(), trace=False):
    nc = build(cfg, debug)
    maps = make_in_maps(cfg, inputs)
    names = set()
    for alloc in nc.allocations:
        pass
    res = run_bass_kernel_spmd(nc, maps, core_ids=list(range(2 * cfg.B)), trace=trace)
    return res


def kernel(**inputs):
    cfg = Cfg()
    res = run(cfg, inputs)
    OWN = cfg.OWN
    outp = np.zeros((cfg.B, 2 * OWN, cfg.D), np.float32)
    for core in range(2 * cfg.B):
        b, half = core // 2, core % 2
        outp[b, half * OWN:(half + 1) * OWN] = np.asarray(res.results[core]['out'])
    return outp
```
